# Optimizing a Trainium2 kernel written in Bass

```python
import math
import jax, jax.numpy as jnp
from jax import lax
import numpy as np

D_MODEL = 1024
BATCH = 8
SEQ = 4096
DEPTH = 1
DEC_BATCH = 2
DEC_SEQ = 16384
PAST_LEN = 128

HEAD_DIM = 64
N_HEADS = 8
KV_HEADS = 2
GROUP = N_HEADS // KV_HEADS
D_ATTN = N_HEADS * HEAD_DIM
D_KV = KV_HEADS * HEAD_DIM
D_CONV = D_MODEL // 2
CONV_GROUPS = 8
D_MIX = D_ATTN + D_CONV
WINDOW = 128
BLOCK = 128
KEY_SPAN = BLOCK + 2 * WINDOW
CONV_K = 31
CONV_PAD = (CONV_K - 1) // 2
PROJ_WIDTH = D_ATTN + 2 * D_KV + D_ATTN + 2 * D_CONV + D_CONV
SPLITS = (D_ATTN, D_ATTN + D_KV, D_ATTN + 2 * D_KV, 2 * D_ATTN + 2 * D_KV,
          2 * D_ATTN + 2 * D_KV + 2 * D_CONV)
EPS = 1e-6
NEG = -1e30

kernel_name = "hymba_swa_conformer_encoder"


def rms_norm(x, g):
    xf = x.astype(jnp.float32)
    y = xf * lax.rsqrt(jnp.mean(xf * xf, axis=-1, keepdims=True) + EPS)
    return (y * g.astype(jnp.float32)).astype(x.dtype)


def layer_norm(x, g, b):
    xf = x.astype(jnp.float32)
    mu = jnp.mean(xf, axis=-1, keepdims=True)
    var = jnp.mean(jnp.square(xf - mu), axis=-1, keepdims=True)
    y = (xf - mu) * lax.rsqrt(var + EPS)
    return (y * g.astype(jnp.float32) + b.astype(jnp.float32)).astype(x.dtype)


def alibi_slopes(n):
    return jnp.asarray([2.0 ** (-8.0 * (h + 1) / n) for h in range(n)], dtype=jnp.float32)


def windowed_gqa(q, k, v, sink):
    B, L, H, Dh = q.shape
    nblk = L // BLOCK
    qb = q.reshape(B, nblk, BLOCK, KV_HEADS, GROUP, Dh)
    kp = jnp.pad(k, ((0, 0), (WINDOW, WINDOW), (0, 0), (0, 0)))
    vp = jnp.pad(v, ((0, 0), (WINDOW, WINDOW), (0, 0), (0, 0)))
    scale = 1.0 / math.sqrt(Dh)
    slopes = alibi_slopes(N_HEADS).reshape(KV_HEADS, GROUP)[:, :, None, None]
    sk = sink.astype(jnp.float32).reshape(KV_HEADS, GROUP)[:, :, None, None]
    qi_idx = jnp.arange(BLOCK)[:, None]
    kj_idx = jnp.arange(KEY_SPAN)[None, :]
    dist = qi_idx - kj_idx + WINDOW
    adist = jnp.abs(dist)
    bias = -slopes * adist.astype(jnp.float32)

    def one_block(i):
        qi = lax.dynamic_index_in_dim(qb, i, axis=1, keepdims=False)
        ki = lax.dynamic_slice_in_dim(kp, i * BLOCK, KEY_SPAN, axis=1)
        vi = lax.dynamic_slice_in_dim(vp, i * BLOCK, KEY_SPAN, axis=1)
        s = jnp.einsum('bqkgd,bskd->bkgqs', qi, ki,
                       preferred_element_type=jnp.float32) * scale + bias
        key_pos = i * BLOCK - WINDOW + kj_idx
        valid = (adist <= WINDOW) & (key_pos >= 0) & (key_pos < L)
        s = jnp.where(valid, s, NEG)
        m = jnp.maximum(jnp.max(s, axis=-1, keepdims=True), sk)
        p = jnp.exp(s - m)
        denom = jnp.sum(p, axis=-1, keepdims=True) + jnp.exp(sk - m)
        w = (p / denom).astype(v.dtype)
        return jnp.einsum('bkgqs,bskd->bqkgd', w, vi)

    out = lax.map(one_block, jnp.arange(nblk))
    return jnp.transpose(out, (1, 0, 2, 3, 4, 5)).reshape(B, L, H * Dh)


def conformer_conv(u, dw_w, dw_b, ln_g, ln_b):
    val, gate = jnp.split(u, 2, axis=-1)
    h = val * jax.nn.sigmoid(gate)
    h = lax.conv_general_dilated(h, dw_w[:, None, :], window_strides=(1,),
                                 padding=[(CONV_PAD, CONV_PAD)],
                                 dimension_numbers=('NWC', 'WIO', 'NWC'),
                                 feature_group_count=D_CONV) + dw_b
    h = layer_norm(h, ln_g, ln_b)
    return jax.nn.silu(h)


def hybrid_layer(x, norm_g, w_in, sink, dw_w, dw_b, ln_g, ln_b, w_out):
    B, L, _ = x.shape
    h = rms_norm(x, norm_g)
    proj = h @ w_in
    q, k, v, g_attn, u_conv, g_conv = jnp.split(proj, SPLITS, axis=-1)
    q = q.reshape(B, L, N_HEADS, HEAD_DIM)
    k = k.reshape(B, L, KV_HEADS, HEAD_DIM)
    v = v.reshape(B, L, KV_HEADS, HEAD_DIM)
    attn = windowed_gqa(q, k, v, sink)
    conv = conformer_conv(u_conv, dw_w, dw_b, ln_g, ln_b)
    mix = jnp.concatenate([attn * jax.nn.silu(g_attn),
                           conv * jax.nn.silu(g_conv)], axis=-1)
    return x + mix @ w_out


def trunk(x, norm_g, w_in, attn_sink, dw_w, dw_b, conv_ln_g, conv_ln_b, w_out, final_g):
    for l in range(DEPTH):
        x = hybrid_layer(x, norm_g[l], w_in[l], attn_sink[l], dw_w[l], dw_b[l],
                         conv_ln_g[l], conv_ln_b[l], w_out[l])
    return rms_norm(x, final_g)


def setup_inputs(seed: int = 0) -> dict:
    key = jax.random.key(seed)
    ks = jax.random.split(key, 12)
    f32 = jnp.float32
    x_prompt = jax.random.normal(ks[0], (BATCH, SEQ, D_MODEL), f32)
    x_sample = jax.random.normal(ks[1], (DEC_BATCH, DEC_SEQ, D_MODEL), f32)
    norm_g = 1.0 + 0.02 * jax.random.normal(ks[2], (DEPTH, D_MODEL), f32)
    w_in = jax.random.normal(ks[3], (DEPTH, D_MODEL, PROJ_WIDTH), f32) * D_MODEL ** -0.5
    attn_sink = 0.5 * jax.random.normal(ks[4], (DEPTH, N_HEADS), f32)
    dw_w = jax.random.normal(ks[5], (DEPTH, CONV_K, D_CONV), f32) * CONV_K ** -0.5
    dw_b = 0.02 * jax.random.normal(ks[6], (DEPTH, D_CONV), f32)
    conv_ln_g = 1.0 + 0.02 * jax.random.normal(ks[7], (DEPTH, D_CONV), f32)
    conv_ln_b = 0.02 * jax.random.normal(ks[8], (DEPTH, D_CONV), f32)
    w_out = jax.random.normal(ks[9], (DEPTH, D_MIX, D_MODEL), f32) * D_MIX ** -0.5
    final_g = 1.0 + 0.02 * jax.random.normal(ks[10], (D_MODEL,), f32)
    return {"x_prompt": x_prompt, "x_sample": x_sample, "norm_g": norm_g, "w_in": w_in,
            "attn_sink": attn_sink, "dw_w": dw_w, "dw_b": dw_b, "conv_ln_g": conv_ln_g,
            "conv_ln_b": conv_ln_b, "w_out": w_out, "final_g": final_g}


def reference(x_prompt, x_sample, norm_g, w_in, attn_sink, dw_w, dw_b, conv_ln_g, conv_ln_b,
              w_out, final_g):
    y_prompt = trunk(x_prompt, norm_g, w_in, attn_sink, dw_w, dw_b, conv_ln_g, conv_ln_b,
                     w_out, final_g)
    y_sample = trunk(x_sample, norm_g, w_in, attn_sink, dw_w, dw_b, conv_ln_g, conv_ln_b,
                     w_out, final_g)
    return (y_prompt, y_sample)
```

```python
from contextlib import ExitStack

import numpy as np
import concourse.bass as bass
import concourse.mybir as mybir
from concourse.alu_op_type import AluOpType as ALU
from concourse.bass_utils import run_bass_kernel_spmd

F32 = mybir.dt.float32
BF16 = mybir.dt.bfloat16
I32 = mybir.dt.int32
AF = mybir.ActivationFunctionType

D = 1024
KC = 8
PW = 2816
NTS = 34
SEGT = NTS * 128
EPS = 1e-6
NEG = -1e30
ENGS = ("pe", "act", "dve", "pool", "sp")


class Sched:
    def __init__(self, nc, stack):
        self.nc = nc
        self.stack = stack
        self.prog = {e: [] for e in ENGS}
        self.sems = {}
        self.cnt = {}
        self.seen = {e: {} for e in ENGS}
        self.lastw = {}
        self.readers = {}
        for e in ("pe", "act", "dve", "pool"):
            self._sem(e)
        self.n_wait = {e: 0 for e in ENGS}
        self.n_inst = {e: 0 for e in ENGS}

    def _sem(self, key):
        if key not in self.sems:
            name = "s_" + "_".join(str(k) for k in (key if isinstance(key, tuple) else (key,)))
            self.sems[key] = self.stack.enter_context(self.nc.semaphore(name))
            self.cnt[key] = 0
        return self.sems[key]

    def _deps(self, eng, reads, writes):
        deps = {}

        def add(ev):
            if ev is None:
                return
            k, v = ev
            if deps.get(k, 0) < v:
                deps[k] = v

        for r in reads:
            excl = isinstance(r, tuple) and r[0] == "ps"
            add(self.lastw.get(r))
            if excl:
                for ev in self.readers.get(r, ()):
                    if ev[0] != eng:
                        add(ev)
        for w in writes:
            ev = self.lastw.get(w)
            if ev is not None and ev[0] != eng:
                add(ev)
            for ev in self.readers.get(w, ()):
                if ev[0] != eng:
                    add(ev)
        out = []
        for k, v in deps.items():
            if k == "pe" and eng == "pe":
                continue
            if self.seen[eng].get(k, 0) >= v:
                continue
            self.seen[eng][k] = v
            out.append((k, v))
        return out

    def _record(self, ev, reads, writes):
        for w in writes:
            self.lastw[w] = ev
            self.readers[w] = []
        for r in reads:
            if r in writes:
                continue
            lst = self.readers.setdefault(r, [])
            lst[:] = [e for e in lst if e[0] != ev[0]]
            lst.append(ev)

    def _emit_waits(self, eng, deps):
        for k, v in deps:
            sem = self.sems[k]
            self.prog[eng].append(I("wait_ge", sem, v))
            self.n_wait[eng] += 1

    def op(self, eng, fn, reads=(), writes=()):
        deps = self._deps(eng, reads, writes)
        self._emit_waits(eng, deps)
        self.cnt[eng] += 1
        v = self.cnt[eng]
        sem = self.sems[eng]
        self.prog[eng].append(lambda h, fn=fn, sem=sem: fn(h).then_inc(sem, 1))
        self.n_inst[eng] += 1
        self._record((eng, v), reads, writes)
        return (eng, v)

    def group(self, eng, fns, reads=(), writes=()):
        deps = self._deps(eng, reads, writes)
        self._emit_waits(eng, deps)
        for fn in fns[:-1]:
            self.prog[eng].append(lambda h, fn=fn: fn(h))
        self.cnt[eng] += 1
        v = self.cnt[eng]
        sem = self.sems[eng]
        last = fns[-1]
        self.prog[eng].append(lambda h, fn=last, sem=sem: fn(h).then_inc(sem, 1))
        self.n_inst[eng] += len(fns)
        self._record((eng, v), reads, writes)
        return (eng, v)

    def dma(self, q, semkey, fn, reads=(), writes=()):
        sem = self._sem(semkey)
        deps = self._deps(q, reads, writes)
        self._emit_waits(q, deps)
        self.cnt[semkey] += 16
        v = self.cnt[semkey]
        self.prog[q].append(lambda h, fn=fn, sem=sem: fn(h).then_inc(sem, 16))
        self.n_inst[q] += 1
        self._record((semkey, v), reads, writes)
        return (semkey, v)

    def wait_all(self, eng, events):
        deps = []
        for k, v in events:
            if self.seen[eng].get(k, 0) < v:
                self.seen[eng][k] = v
                deps.append((k, v))
        self._emit_waits(eng, deps)

    def finish(self):
        nc = self.nc
        with nc.Block() as block:
            @block.sync
            def _(h):
                for f in self.prog["sp"]:
                    f(h)

            @block.tensor
            def _(h):
                for f in self.prog["pe"]:
                    f(h)

            @block.scalar
            def _(h):
                for f in self.prog["act"]:
                    f(h)

            @block.vector
            def _(h):
                for f in self.prog["dve"]:
                    f(h)

            @block.gpsimd
            def _(h):
                for f in self.prog["pool"]:
                    f(h)


def I(method, *a, **k):
    return lambda h: getattr(h, method)(*a, **k)


def build_nc(n_seg=2, n_st=8):
    nc = bass.Bass("TRN2", target_bir_lowering=False)

    def din(name, shape, dt=F32):
        return nc.dram_tensor(name, list(shape), dt, kind="ExternalInput").ap()

    xT_d = din("xT", [2, 128, KC, SEGT])
    xtok_d = din("xtok", [2, 4096, D])
    kbias_d = din("kbias", [128, 2 * NTS])
    win_d = din("w_in", [128, KC, PW])
    ng_d = din("ng", [128, KC])
    wout_d = din("w_out", [128, KC, D])
    dww_d = din("dww", [128, 4, 31])
    cvec_d = din("cvec", [128, 20])
    sink_d = din("sink", [1, 8])
    fg_d = din("fg", [1, D])
    ident_d = din("ident", [128, 128])
    id32_d = din("id32", [128, 32])
    y_d = nc.dram_tensor("y", [2, 4096, D], F32, kind="ExternalOutput").ap()

    with ExitStack() as st:
        S = Sched(nc, st)

        def sb(name, shape, dt):
            return st.enter_context(nc.sbuf_tensor("sb_" + name, list(shape), dt))

        Wp = sb("Wp", [128, KC, PW], BF16)
        Wo = sb("Wo", [128, KC, D], BF16)
        DG = sb("DG", [128, 4, 31, 32], BF16)
        ET = sb("ET", [128, 3, 2, 512], BF16)
        identb = sb("identb", [128, 128], BF16)
        onesb = sb("onesb", [128, 128], BF16)
        fgb = sb("fgb", [128, D], F32)
        ng = sb("ng", [128, KC], F32)
        dww = sb("dww", [128, 4, 31], F32)
        cvec = sb("cvec", [128, 20], F32)
        esink = sb("esink", [128, 8], F32)
        kbias = sb("kbias", [128, 2 * NTS], F32)
        nhalf = sb("nhalf", [128, 1], F32)
        id32 = sb("id32", [128, 32], F32)

        XT = sb("XT", [128, KC, 512], F32)
        XN = sb("XN", [128, KC, 512], BF16)
        XSQ = sb("XSQ", [128, KC, 512], BF16)
        VBC = sb("VBC", [128, 512], F32)
        RBC = sb("RBC", [128, 512], F32)
        QT = sb("QT", [128, 4, 640], BF16)
        KT = sb("KT", [128, 8, 128], BF16)
        VP = sb("VP", [128, 8, 2, 65], BF16)
        HB = sb("HB", [128, 4, 784], BF16)
        SGC = sb("SGC", [128, 4, 640], BF16)
        SGA = sb("SGA", [128, 8, 512], BF16)
        TH = sb("TH", [128, 2, 512], F32)
        TH2 = sb("TH2", [128, 2, 512], F32)
        PT = sb("PT", [128, 2, 3, 512], BF16)
        MA = sb("MA", [128, 2, 512], BF16)
        TO = sb("TO", [128, 2, 256], F32)
        DEN = sb("DEN", [128, 2, 8], F32)
        MT = sb("MT", [128, KC, 512], BF16)
        YB = sb("YB", [128, 2, 512], BF16)
        YSQ = sb("YSQ", [128, 2, 512], BF16)
        MU = sb("MU", [128, 512], F32)
        VR = sb("VR", [128, 512], F32)
        T1 = sb("T1", [128, 2, 512], F32)
        LL = sb("LL", [128, 2, 512], F32)
        XTOK = sb("XTOK", [128, 3, D], F32)
        NWA = sb("NWA", [128, 512], F32)
        NWT = sb("NWT", [128, 512], I32)
        RR = sb("RR", [128, 512], F32)
        SS = sb("SS", [128, 4], F32)

        Df = TH2[:, 0, 0:128]
        Dabs = TH2[:, 0, 128:256]
        Mle = TH2[:, 0, 256:384]
        Mge = TH2[:, 0, 384:512]
        Etmp = TH2[:, 1, 0:128]
        identf = TH2[:, 1, 128:256]
        Di = TH2[:, 1, 256:384].bitcast(I32)
        JUNK = YB[:, :, :].rearrange("p a b -> p (a b)")
        PS = [st.enter_context(nc.psum_tensor("ps%d" % i, [128, 512], F32)) for i in range(8)]
        bank_ctr = [0]

        def nb():
            b = bank_ctr[0] % 8
            bank_ctr[0] += 1
            return b

        def pk(b):
            return ("ps", b)

        def ld(key, dst, src, wkey):
            S.dma("sp", key, I("dma_start", out=dst, in_=src), writes=[wkey])

        ld("c_ng", ng[:], ng_d, "ng")
        ld("c_dww", dww[:], dww_d, "dww")
        ld("c_cvec", cvec[:], cvec_d, "cvec")
        ld("c_kb", kbias[:], kbias_d, "kbias")
        ld("c_id", identf, ident_d, "identf")
        ld("c_id32", id32[:], id32_d, "id32")
        S.dma("sp", "c_sink", I("dma_start", out=esink[:], in_=sink_d.to_broadcast([128, 8])),
              writes=["esink"])
        S.dma("sp", "c_fg", I("dma_start", out=fgb[:], in_=fg_d.to_broadcast([128, D])),
              writes=["fgb"])

        S.op("pool", I("memset", nhalf[:], -0.5), writes=["nhalf"])
        S.op("pool", I("memset", onesb[:], 1.0), writes=["onesb"])
        S.op("pool", I("memset", VP[:, :, :, 64:65], 1.0), writes=[("VP", s) for s in range(8)])
        S.op("dve", I("tensor_copy", out=identb[:], in_=identf), reads=["identf"], writes=["identb"])
        S.op("act", I("activation", out=esink[:], in_=esink[:], func=AF.Exp),
             reads=["esink"], writes=["esink"])
        S.op("dve", I("tensor_scalar", out=cvec[:, 12:20], in0=cvec[:, 4:12], scalar1=0.5, scalar2=None,
                                              op0=ALU.mult), reads=["cvec"], writes=["cvec"])

        S.op("pool", I("iota", Di, pattern=[[1, 128]], base=0, channel_multiplier=-1), writes=["Di"])
        S.op("dve", I("tensor_copy", out=Df, in_=Di), reads=["Di"], writes=["Df"])
        S.op("act", I("activation", out=Dabs, in_=Df, func=AF.Abs), reads=["Df"], writes=["Dabs"])
        S.op("dve", I("tensor_scalar", out=Mle, in0=Df, scalar1=0.0, scalar2=None, op0=ALU.is_le),
             reads=["Df"], writes=["Mle"])
        S.op("dve", I("tensor_scalar", out=Mge, in0=Df, scalar1=0.0, scalar2=None, op0=ALU.is_ge),
             reads=["Df"], writes=["Mge"])
        for hd in range(8):
            m = 2.0 ** (-(hd + 1))
            g, a = hd // 4, hd % 4
            cs = slice(a * 128, (a + 1) * 128)
            S.op("act", I("activation", out=Etmp, in_=Df, func=AF.Exp, scale=-m, bias=-128.0 * m),
                 reads=["Df"], writes=["Etmp"])
            S.op("dve", I("tensor_tensor", out=ET[:, 0, g, cs], in0=Etmp, in1=Mle, op=ALU.mult),
                 reads=["Etmp", "Mle"], writes=["ET"])
            S.op("act", I("activation", out=ET[:, 1, g, cs], in_=Dabs, func=AF.Exp, scale=-m),
                 reads=["Dabs"], writes=["ET"])
            S.op("act", I("activation", out=Etmp, in_=Df, func=AF.Exp, scale=m, bias=-128.0 * m),
                 reads=["Df"], writes=["Etmp"])
            S.op("dve", I("tensor_tensor", out=ET[:, 2, g, cs], in0=Etmp, in1=Mge, op=ALU.mult),
                 reads=["Etmp", "Mge"], writes=["ET"])

        XTf = XT[:, 0:6, :].rearrange("p a b -> p (a b)")
        XKf = XTOK[:, :, :].rearrange("p a b -> p (a b)")
        for kc in range(KC):
            stg, skey, eng, dkey = (XTf, "XT", "dve", "w_st0") if kc % 2 == 0 else (XKf, "XTOKall", "pool", "w_st1")
            S.dma("sp", dkey, I("dma_start", out=stg[:, 0:PW], in_=win_d[:, kc, :]), writes=[skey])
            S.op(eng, I("tensor_scalar", out=Wp[:, kc, 0:512], in0=stg[:, 0:512], scalar1=ng[:, kc:kc + 1],
                        scalar2=0.125, op0=ALU.mult, op1=ALU.mult), reads=[skey, "ng"], writes=["Wp"])
            S.op(eng, I("tensor_scalar", out=Wp[:, kc, 512:PW], in0=stg[:, 512:PW], scalar1=ng[:, kc:kc + 1],
                        scalar2=1.0, op0=ALU.mult, op1=ALU.mult), reads=[skey, "ng"], writes=["Wp"])
        for hf in range(2):
            S.dma("sp", "w_st0", I("dma_start", out=XT[:, :, :].rearrange("p a b -> p (a b)").rearrange("p (k c) -> p k c", c=D),
                                   in_=wout_d[:, 4 * hf:4 * hf + 4, :]), writes=["XT"])
            S.op("dve", I("tensor_scalar", out=Wo[:, 4 * hf:4 * hf + 4, :],
                          in0=XT[:, :, :].rearrange("p a b -> p (a b)").rearrange("p (k c) -> p k c", c=D),
                          scalar1=(0.5 if hf == 0 else 0.25), scalar2=None, op0=ALU.mult), reads=["XT"], writes=["Wo"])
        for c in range(4):
            S.op("pool", I("tensor_tensor", out=DG[:, c, :, :], in0=id32[:, :].unsqueeze(1).to_broadcast([128, 31, 32]),
                           in1=dww[:, c, :].unsqueeze(2).to_broadcast([128, 31, 32]), op=ALU.mult),
                 reads=["id32", "dww"], writes=["DG"])
        def rsqrt_bc(R, rk, V, vk, N):
            S.op("dve", I("tensor_scalar", out=NWT[:, 0:N], in0=V.bitcast(I32), scalar1=1, scalar2=None,
                          op0=ALU.arith_shift_right), reads=[vk], writes=["NWT"])
            S.op("dve", I("tensor_scalar", out=R.bitcast(I32), in0=NWT[:, 0:N], scalar1=-1.0, scalar2=1597463007.0,
                          op0=ALU.mult, op1=ALU.add), reads=["NWT"], writes=[rk])
            for _ in range(2):
                S.op("dve", I("tensor_tensor", out=NWA[:, 0:N], in0=R, in1=R, op=ALU.mult), reads=[rk], writes=["NWA"])
                S.op("dve", I("scalar_tensor_tensor", out=NWA[:, 0:N], in0=NWA[:, 0:N], scalar=-0.5, in1=V,
                              op0=ALU.mult, op1=ALU.mult), reads=["NWA", vk], writes=["NWA"])
                S.op("dve", I("scalar_tensor_tensor", out=R, in0=NWA[:, 0:N], scalar=1.5, in1=R,
                              op0=ALU.add, op1=ALU.mult), reads=["NWA", rk], writes=[rk])

        def slot(tile):
            return (tile - 1) % 8

        pool = {"banks": list(range(8)), "i": 0}

        def nb():
            lst = pool["banks"]
            b = lst[pool["i"] % len(lst)]
            pool["i"] += 1
            return b

        def set_pool(lst):
            pool["banks"] = list(lst)
            pool["i"] = 0

        thc = [0]

        def th_slot():
            thc[0] += 1
            return thc[0] % 2

        xtc = [0]
        ssc = [0]
        stores = {}

        def prep_early(seg, tok0, T):
            def s_a():
                S.dma("sp", "xT", I("dma_start", out=XT[:, :, 0:T], in_=xT_d[seg, :, :, tok0:tok0 + T]),
                      writes=["XT"])
                S.op("act", I("activation", out=XSQ[:, :, 0:T], in_=XT[:, :, 0:T], func=AF.Square),
                     reads=["XT"], writes=["XSQ"])

            def s_b():
                b = nb()
                S.group("pe", [I("matmul", PS[b][:, 0:T], lhsT=onesb[:], rhs=XSQ[:, kc, 0:T],
                                 start=(kc == 0), stop=(kc == KC - 1)) for kc in range(KC)],
                        reads=["XSQ", "onesb"], writes=[pk(b)])
                S.op("act", I("activation", out=VBC[:, 0:T], in_=PS[b][:, 0:T], func=AF.Identity,
                              scale=1.0 / D, bias=EPS), reads=[pk(b)], writes=["VBC"])

            def s_c():
                rsqrt_bc(RBC[:, 0:T], "RBC", VBC[:, 0:T], "VBC", T)
            return [s_a, s_b, s_c]

        def prep_xn(T):
            def s_x(k0):
                def f():
                    for kc in range(k0, k0 + 2):
                        S.op("dve", I("tensor_tensor", out=XN[:, kc, 0:T], in0=XT[:, kc, 0:T], in1=RBC[:, 0:T],
                                      op=ALU.mult), reads=["XT", "RBC"], writes=["XN"])
                return f
            return [s_x(k0) for k0 in range(0, KC, 2)]

        def fm_chunk(c0, T):
            b = nb()
            S.group("pe", [I("matmul", PS[b][:, 0:T], lhsT=Wp[:, kc, c0:c0 + 128], rhs=XN[:, kc, 0:T],
                             start=(kc == 0), stop=(kc == KC - 1)) for kc in range(KC)],
                    reads=["XN", "Wp"], writes=[pk(b)])
            return b

        def a_steps(seg, tile0, nt, full, qcol, hcol, shift=True):
            T = nt * 128
            main, gcs = [], []

            def s_shift():
                S.op("act", I("activation", out=QT[:, :, 0:128], in_=QT[:, :, 512:640], func=AF.Copy),
                     reads=["QT"], writes=["QT"])
                S.op("act", I("activation", out=HB[:, :, 0:144], in_=HB[:, :, 512:656], func=AF.Copy),
                     reads=["HB"], writes=["HB"])
            if full and shift:
                main.append(s_shift)

            def s_q(a):
                def f():
                    b = fm_chunk(a * 128, T)
                    S.op("act", I("activation", out=QT[:, a, qcol:qcol + T], in_=PS[b][:, 0:T], func=AF.Copy),
                         reads=[pk(b)], writes=["QT"])
                return f

            def s_k():
                b = fm_chunk(512, T)
                for i in range(nt):
                    s = slot(tile0 + i)
                    S.op("act", I("activation", out=KT[:, s, :], in_=PS[b][:, i * 128:(i + 1) * 128], func=AF.Copy),
                         reads=[pk(b)], writes=[("KT", s)])

            def s_glu(c):
                def f():
                    bv = fm_chunk(1280 + c * 128, T)
                    bg = fm_chunk(1792 + c * 128, T)
                    ts = th_slot()
                    S.op("act", I("activation", out=TH[:, ts, 0:T], in_=PS[bg][:, 0:T], func=AF.Tanh, scale=0.5),
                         reads=[pk(bg)], writes=[("TH", ts)])
                    S.op("dve", I("scalar_tensor_tensor", out=HB[:, c, hcol:hcol + T], in0=TH[:, ts, 0:T], scalar=1.0,
                                  in1=PS[bv][:, 0:T], op0=ALU.add, op1=ALU.mult),
                         reads=[("TH", ts), pk(bv)], writes=["HB"])
                return f

            def s_gc(c):
                def f():
                    b = fm_chunk(2304 + c * 128, T)
                    ts = th_slot()
                    S.op("act", I("activation", out=TH[:, ts, 0:T], in_=PS[b][:, 0:T], func=AF.Tanh, scale=0.5),
                         reads=[pk(b)], writes=[("TH", ts)])
                    S.op("dve", I("scalar_tensor_tensor", out=SGC[:, c, qcol:qcol + T], in0=TH[:, ts, 0:T], scalar=1.0,
                                  in1=PS[b][:, 0:T], op0=ALU.add, op1=ALU.mult),
                         reads=[("TH", ts), pk(b)], writes=["SGC"])
                return f

            def s_v():
                bV = nb()
                for i in range(nt):
                    S.group("pe", [I("matmul", PS[bV][:, i * 128:(i + 1) * 128], lhsT=XN[:, kc, i * 128:(i + 1) * 128],
                                     rhs=Wp[:, kc, 640:768], start=(kc == 0), stop=(kc == KC - 1))
                                   for kc in range(KC)], reads=["XN", "Wp"], writes=[pk(bV)])
                for i in range(nt):
                    s = slot(tile0 + i)
                    S.op("act", I("activation", out=VP[:, s, :, 0:64],
                                  in_=PS[bV][:, i * 128:(i + 1) * 128].rearrange("p (g d) -> p g d", g=2),
                                  func=AF.Copy), reads=[pk(bV)], writes=[("VP", s)])

            def s_ga(i):
                def f():
                    s = slot(tile0 + i)
                    b = nb()
                    S.group("pe", [I("matmul", PS[b][:, :], lhsT=XN[:, kc, i * 128:(i + 1) * 128],
                                     rhs=Wp[:, kc, 768:1280], start=(kc == 0), stop=(kc == KC - 1))
                                   for kc in range(KC)], reads=["XN", "Wp"], writes=[pk(b)])
                    ts = th_slot()
                    S.op("act", I("activation", out=TH[:, ts, :], in_=PS[b][:, :], func=AF.Tanh, scale=0.5),
                         reads=[pk(b)], writes=[("TH", ts)])
                    S.op("dve", I("scalar_tensor_tensor", out=SGA[:, s, :], in0=TH[:, ts, :], scalar=1.0,
                                  in1=PS[b][:, :], op0=ALU.add, op1=ALU.mult),
                         reads=[("TH", ts), pk(b)], writes=[("SGA", s)])
                return f

            main.append(s_k)
            main.append(s_v)
            if full:
                for a in range(4):
                    main.append(s_q(a))
                for i in range(nt):
                    main.append(s_ga(i))
            for c in range(4):
                f_ = s_glu(c)
                f_._w = 3.4
                main.append(f_)

            def s_shift_gc():
                S.op("act", I("activation", out=SGC[:, :, 0:128], in_=SGC[:, :, 512:640], func=AF.Copy),
                     reads=["SGC"], writes=["SGC"])
            if full:
                if shift:
                    gcs.append(s_shift_gc)
                for c in range(4):
                    gcs.append(s_gc(c))
            return main, gcs

        BS = [4, 5, 6]
        BO = 7

        def att_steps(seg, qt0, nq, qcol):
            steps = []

            def half_a(i, g):
                def f():
                    qt = qt0 + i
                    qc = qcol + i * 128
                    rows = slice(64 * g, 64 * g + 64)
                    for kb in range(3):
                        ks = slot(qt - 1 + kb)
                        S.op("pe", I("matmul", PS[BS[kb]][:, :], lhsT=KT[rows, ks, :], rhs=QT[rows, :, qc:qc + 128],
                                     start=True, stop=True), reads=[("KT", ks), "QT"], writes=[pk(BS[kb])])
                    for kb in range(3):
                        kcol = seg * NTS + qt - 1 + kb
                        S.op("act", I("activation", out=PT[:, g, kb, :], in_=PS[BS[kb]][:, :], func=AF.Exp,
                                      bias=kbias[:, kcol:kcol + 1]), reads=[pk(BS[kb]), "kbias"], writes=[("PT", g, kb)])
                        S.op("dve", I("tensor_tensor", out=PT[:, g, kb, :], in0=PT[:, g, kb, :], in1=ET[:, kb, g, :],
                                      op=ALU.mult), reads=[("PT", g, kb), "ET"], writes=[("PT", g, kb)])
                return f

            def half_b(i, g):
                def f():
                    qt = qt0 + i
                    ms = i % 2
                    fns = []
                    for a in range(4):
                        for kb in range(3):
                            ks = slot(qt - 1 + kb)
                            fns.append(I("matmul", PS[BO][:, a * 65:(a + 1) * 65], lhsT=PT[:, g, kb, a * 128:(a + 1) * 128],
                                         rhs=VP[:, ks, g, :], start=(kb == 0), stop=(kb == 2)))
                    S.group("pe", fns, reads=[("PT", g, kb) for kb in range(3)] +
                            [("VP", slot(qt - 1 + kb)) for kb in range(3)], writes=[pk(BO)])
                    Ov = PS[BO][:, 0:260].rearrange("p (a e) -> p a e", a=4)
                    S.op("dve", I("tensor_tensor", out=DEN[:, g, 0:4], in0=Ov[:, :, 64], in1=esink[:, 4 * g:4 * g + 4],
                                  op=ALU.add), reads=[pk(BO), "esink"], writes=[("DEN", g)])
                    S.op("dve", I("reciprocal", out=DEN[:, g, 4:8], in_=DEN[:, g, 0:4]),
                         reads=[("DEN", g)], writes=[("DEN", g)])
                    S.op("dve", I("tensor_tensor", out=TO[:, g, :].rearrange("p (a d) -> p a d", a=4), in0=Ov[:, :, 0:64],
                                  in1=DEN[:, g, 4:8].unsqueeze(2).to_broadcast([128, 4, 64]), op=ALU.mult),
                         reads=[pk(BO), ("DEN", g)], writes=[("TO", g)])
                    S.op("dve", I("tensor_tensor", out=MA[:, ms, 256 * g:256 * g + 256], in0=TO[:, g, :],
                                  in1=SGA[:, slot(qt), 256 * g:256 * g + 256], op=ALU.mult),
                         reads=[("TO", g), ("SGA", slot(qt))], writes=[("MA", ms, g)])
                return f

            def half_c(i):
                def f():
                    ms = i % 2
                    Tv = PS[BO][:].bitcast(BF16)
                    S.group("pe", [I("transpose", Tv[:, c * 128:(c + 1) * 128], MA[:, ms, c * 128:(c + 1) * 128],
                                     identb[:]) for c in range(4)],
                            reads=[("MA", ms, 0), ("MA", ms, 1), "identb"], writes=[pk(BO)])
                    S.op("act", I("activation", out=MT[:, 0:4, i * 128:(i + 1) * 128],
                                  in_=Tv[:, 0:512].rearrange("p (c q) -> p c q", c=4), func=AF.Copy),
                         reads=[pk(BO)], writes=[("MTa", i)])
                return f
            units = [(i, g) for i in range(nq) for g in range(2)]
            n = len(units)
            for u, (i, g) in enumerate(units):
                steps.append(half_a(i, g))
                if u >= 3 and u % 2 == 1:
                    steps.append(half_c((u - 3) // 2))
                if u >= 1:
                    steps.append(half_b(*units[u - 1]))
            steps.append(half_b(*units[-1]))
            if nq >= 2:
                pass
            done = set((u - 3) // 2 for u in range(n) if u >= 3 and u % 2 == 1)
            for i in range(nq):
                if i not in done:
                    steps.append(half_c(i))
            return steps

        BY = [0, 1, 2, 3]

        def conv_steps(hcol, N, npieces=8):
            taps = [(c, j) for j in range(31) for c in range(4)]
            per = (len(taps) + npieces - 1) // npieces
            steps = []

            def piece(lst):
                def f():
                    fns = []
                    for c, j in lst:
                        for q4 in range(4):
                            r = slice(32 * q4, 32 * q4 + 32)
                            fns.append(I("matmul", PS[BY[q4]][32 * c:32 * c + 32, 0:N], lhsT=DG[r, c, j, :],
                                         rhs=HB[r, c, hcol + j - 15:hcol + j - 15 + N],
                                         start=(j == 0), stop=(j == 30), tile_position=(32 * q4, 32 * c)))
                    S.group("pe", fns, reads=["HB", "DG"], writes=[pk(b) for b in BY])
                return f
            for k in range(0, len(taps), per):
                steps.append(piece(taps[k:k + per]))
            return steps

        def ln_steps(N, qcol):
            b1, b2 = 4, 5
            steps = []

            def stat_a(be):
                def f():
                    sl = be % 2
                    S.op("act", I("activation", out=YB[:, sl, 0:N], in_=PS[BY[be]][:, 0:N], func=AF.Identity,
                                  bias=cvec[:, be:be + 1]), reads=[pk(BY[be]), "cvec"], writes=[("YB", sl)])
                    S.op("act", I("activation", out=YSQ[:, sl, 0:N], in_=PS[BY[be]][:, 0:N], func=AF.Square,
                                  bias=cvec[:, be:be + 1]), reads=[pk(BY[be]), "cvec"], writes=[("YSQ", sl)])
                return f

            def stat_b(be):
                def f():
                    sl = be % 2
                    S.op("pe", I("matmul", PS[b1][:, 0:N], lhsT=onesb[:], rhs=YB[:, sl, 0:N], start=(be == 0),
                                 stop=(be == 3)), reads=[("YB", sl), "onesb"], writes=[pk(b1)])
                    S.op("pe", I("matmul", PS[b2][:, 0:N], lhsT=onesb[:], rhs=YSQ[:, sl, 0:N], start=(be == 0),
                                 stop=(be == 3)), reads=[("YSQ", sl), "onesb"], writes=[pk(b2)])
                return f

            def fin_a():
                S.op("act", I("activation", out=MU[:, 0:N], in_=PS[b1][:, 0:N], func=AF.Identity, scale=1.0 / 512),
                     reads=[pk(b1)], writes=["MU"])
                S.op("act", I("activation", out=VR[:, 0:N], in_=PS[b2][:, 0:N], func=AF.Identity, scale=1.0 / 512,
                              bias=EPS), reads=[pk(b2)], writes=["VR"])
                pool["banks"] = pool["banks"] + [b1, b2]

            def fin_d():
                S.op("dve", I("tensor_tensor", out=NWA[:, 0:N], in0=MU[:, 0:N], in1=MU[:, 0:N], op=ALU.mult),
                     reads=["MU"], writes=["NWA"])
                S.op("dve", I("tensor_tensor", out=VR[:, 0:N], in0=VR[:, 0:N], in1=NWA[:, 0:N], op=ALU.subtract),
                     reads=["VR", "NWA"], writes=["VR"])
                rsqrt_bc(RR[:, 0:N], "RR", VR[:, 0:N], "VR", N)

            tsl = {}

            def t1_d(be):
                def f():
                    sl = be % 2
                    S.op("dve", I("scalar_tensor_tensor", out=T1[:, sl, 0:N], in0=PS[BY[be]][:, 0:N],
                                  scalar=cvec[:, be:be + 1], in1=MU[:, 0:N], op0=ALU.add, op1=ALU.subtract),
                         reads=[pk(BY[be]), "cvec", "MU"], writes=[("T1", sl)])
                    S.op("dve", I("tensor_tensor", out=T1[:, sl, 0:N], in0=T1[:, sl, 0:N], in1=RR[:, 0:N], op=ALU.mult),
                         reads=[("T1", sl), "RR"], writes=[("T1", sl)])
                return f

            def t1_a(be):
                def f():
                    sl = be % 2
                    ts = be % 2
                    tsl[be] = ts
                    S.op("act", I("activation", out=LL[:, sl, 0:N], in_=T1[:, sl, 0:N], func=AF.Identity,
                                  scale=cvec[:, 4 + be:5 + be], bias=cvec[:, 8 + be:9 + be]),
                         reads=[("T1", sl), "cvec"], writes=[("LL", sl)])
                    S.op("act", I("activation", out=TH2[:, ts, 0:N], in_=T1[:, sl, 0:N], func=AF.Tanh,
                                  scale=cvec[:, 12 + be:13 + be], bias=cvec[:, 16 + be:17 + be]),
                         reads=[("T1", sl), "cvec"], writes=[("TH2", ts)])
                return f

            def t1_z(be):
                def f():
                    sl = be % 2
                    ts = tsl[be]
                    S.op("dve", I("scalar_tensor_tensor", out=LL[:, sl, 0:N], in0=TH2[:, ts, 0:N], scalar=1.0,
                                  in1=LL[:, sl, 0:N], op0=ALU.add, op1=ALU.mult),
                         reads=[("TH2", ts), ("LL", sl)], writes=[("LL", sl)])
                    S.op("dve", I("tensor_tensor", out=MT[:, 4 + be, 0:N], in0=LL[:, sl, 0:N],
                                  in1=SGC[:, be, qcol:qcol + N], op=ALU.mult),
                         reads=[("LL", sl), "SGC"], writes=[("MTc", be)])
                return f
            steps += [stat_a(0), stat_a(1), stat_b(0), stat_a(2), stat_b(1), stat_a(3), stat_b(2), stat_b(3)]
            steps += [fin_a, fin_d]
            steps += [t1_d(0), t1_d(1), t1_a(0), t1_d(2), t1_a(1), t1_z(0), t1_d(3), t1_a(2), t1_z(1), t1_a(3),
                      t1_z(2), t1_z(3)]
            return steps

        def outp_steps(seg, qt0, nq):
            tokbase = (qt0 - 1) * 128
            xsl = {}

            def load_x(i):
                sl = xtc[0] % 3
                xtc[0] += 1
                xsl[i] = sl
                t0 = tokbase + i * 128
                S.dma("sp", ("xl", sl), I("dma_start", out=XTOK[:, sl, :], in_=xtok_d[seg, t0:t0 + 128, :]),
                      writes=[("XTOK", sl), "XTOKall"] if xtc[0] <= 3 else [("XTOK", sl)])

            def tile(i):
                def f():
                    if i == 0:
                        load_x(0)
                    if i + 1 < nq:
                        load_x(i + 1)
                    sl = xsl[i]
                    t0 = tokbase + i * 128
                    bc = [nb(), nb()]
                    for hf in range(2):
                        S.group("pe", [I("matmul", PS[bc[hf]][:, :], lhsT=MT[:, kc, i * 128:(i + 1) * 128],
                                         rhs=Wo[:, kc, hf * 512:(hf + 1) * 512], start=(kc == 0), stop=(kc == KC - 1))
                                       for kc in range(KC)],
                                reads=[("MTa", i)] + [("MTc", be) for be in range(4)] + ["Wo"], writes=[pk(bc[hf])])
                    for hf in range(2):
                        S.op("dve", I("tensor_tensor", out=XTOK[:, sl, hf * 512:(hf + 1) * 512], in0=PS[bc[hf]][:, :],
                                      in1=XTOK[:, sl, hf * 512:(hf + 1) * 512], op=ALU.add),
                             reads=[pk(bc[hf]), ("XTOK", sl)], writes=[("XTOK", sl)])
                    sc = ssc[0] % 4
                    ssc[0] += 1
                    S.op("act", I("activation", out=JUNK, in_=XTOK[:, sl, :], func=AF.Square,
                                  accum_out=SS[:, sc:sc + 1]), reads=[("XTOK", sl)], writes=[("YB", 0), ("YB", 1), ("SS", sc)])
                    S.op("dve", I("tensor_scalar", out=SS[:, sc:sc + 1], in0=SS[:, sc:sc + 1], scalar1=1.0 / D,
                                  scalar2=EPS, op0=ALU.mult, op1=ALU.add), reads=[("SS", sc)], writes=[("SS", sc)])
                    S.op("pool", I("tensor_tensor", out=SS[:, sc:sc + 1], in0=SS[:, sc:sc + 1], in1=nhalf[:, 0:1],
                                   op=ALU.pow), reads=[("SS", sc), "nhalf"], writes=[("SS", sc)])
                    S.op("dve", I("scalar_tensor_tensor", out=XTOK[:, sl, :], in0=XTOK[:, sl, :], scalar=SS[:, sc:sc + 1],
                                  in1=fgb[:], op0=ALU.mult, op1=ALU.mult),
                         reads=[("XTOK", sl), ("SS", sc), "fgb"], writes=[("XTOK", sl)])
                    ev = S.dma("sp", ("st", sl), I("dma_start", out=y_d[seg, t0:t0 + 128, :], in_=XTOK[:, sl, :]),
                               reads=[("XTOK", sl)], writes=[])
                    stores[("st", sl)] = ev
                return f
            return [tile(i) for i in range(nq)]

        def run(steps):
            for f in steps:
                f()

        def merge_w(la, wa, lb, wb):
            if not la:
                return list(lb)
            if not lb:
                return list(la)
            def mids(w):
                tot = float(sum(w))
                out, c = [], 0.0
                for x in w:
                    out.append((c + 0.5 * x) / tot)
                    c += x
                return out
            ma, mb = mids(wa), mids(wb)
            out, i, k = [], 0, 0
            while i < len(la) or k < len(lb):
                if k >= len(lb) or (i < len(la) and ma[i] <= mb[k]):
                    out.append(la[i]); i += 1
                else:
                    out.append(lb[k]); k += 1
            return out

        def merge(*lists):
            lists = [l for l in lists if l]
            out = []
            pos = [0] * len(lists)
            total = sum(len(l) for l in lists)
            for _ in range(total):
                best, bi = None, None
                for k, l in enumerate(lists):
                    if pos[k] < len(l):
                        frac = (pos[k] + 0.5) / len(l)
                        if best is None or frac < best:
                            best, bi = frac, k
                out.append(lists[bi][pos[bi]])
                pos[bi] += 1
            return out

        def unit_prep(u):
            kind, sg, jj = u
            if kind == "st":
                return prep_early(sg, 128 + 512 * jj, 512), 512
            if kind == "haloR":
                return prep_early(sg, 128 + 512 * n_st, 128), 128
            if kind == "haloL":
                return prep_early(sg, 0, 128), 128
            return prep_early(sg, 128, 512), 512

        def unit_a(u):
            kind, sg, jj = u
            if kind == "st":
                return a_steps(sg, 4 * jj + 1, 4, True, 128, 144)
            if kind == "haloR":
                return a_steps(sg, 4 * n_st + 1, 1, False, None, 656)
            if kind == "haloL":
                return a_steps(sg, 0, 1, False, None, 16)
            raise AssertionError(kind)

        t_pending = None
        for seg in range(n_seg):
            last = 4 * n_st
            set_pool(range(8))
            if seg == 0:
                run(prep_early(seg, 0, 128) + prep_xn(128))
                m, _ = a_steps(seg, 0, 1, False, None, 16)
                run(merge(m, prep_early(seg, 128, 512)))
            run(prep_xn(512))
            m, g = a_steps(seg, 1, 4, True, 128, 144, shift=False)
            units = [("st", seg, jj) for jj in range(1, n_st)] + [("haloR", seg, None)]
            if seg + 1 < n_seg:
                units += [("haloL", seg + 1, None), ("st0", seg + 1, None)]
            pe0, t_pending = unit_prep(units[0])
            run(merge(m + g, pe0))
            for j in range(n_st + 1):
                if j == 0:
                    win = (1, 3, 128, 144)
                elif j < n_st:
                    win = (4 * j, 4, 0, 16)
                else:
                    win = (last, 1, 512, 528)
                qt0, nq, qcol, hcol = win
                N = nq * 128
                nxt = units[j] if j < len(units) else None
                nxt2 = units[j + 1] if j + 1 < len(units) else None
                if nxt is not None:
                    nxt_xn = prep_xn(t_pending)
                    nxt_a, nxt_g = unit_a(nxt)
                else:
                    nxt_xn, nxt_a, nxt_g = [], [], []
                if nxt2 is not None:
                    nxt2_prep, t_pending = unit_prep(nxt2)
                else:
                    nxt2_prep = []
                att = att_steps(seg, qt0, nq, qcol)
                ntail = min(len(att), 2 + (1 if nq >= 2 else 0))
                att_head, att_tail = att[:len(att) - ntail], att[len(att) - ntail:]
                p1 = merge(att_head, conv_steps(hcol, N))
                for r, stp in enumerate(nxt_xn):
                    p1.insert(min(len(p1), 1 + 2 * r), stp)
                run(p1)
                set_pool([6, 7])
                if nxt is not None and nxt[0] == "haloL":
                    run(att_tail)
                    att_tail = []
                lnl = ln_steps(N, qcol)
                w_ln = [1.5] * max(0, len(att_tail) - 1) + [1.0] * min(1, len(att_tail)) + \
                       [1.2, 1.2, 0.4, 1.2, 0.4, 1.2, 0.4, 0.4, 1.2, 7.0] + [1.2] * 12
                assert len(w_ln) == len(att_tail) + len(lnl)
                w_a = [getattr(stp, "_w", 1.7) for stp in nxt_a]
                hh = len(nxt_a) // 2
                a_stream = nxt_a[:hh] + nxt2_prep + nxt_a[hh:]
                w_stream = w_a[:hh] + [0.3, 1.7, 0.3][:len(nxt2_prep)] + w_a[hh:]
                run(merge_w(att_tail + lnl + [lambda: None], w_ln + [8.0], a_stream, w_stream))
                set_pool([4, 5, 6, 7])
                run(merge(outp_steps(seg, qt0, nq), nxt_g))

        S.wait_all("sp", list(stores.values()))
        S.finish()
        nc._sched_stats = (S.n_inst, S.n_wait)
    return nc


def _perm_bank():
    p = np.arange(128)
    return np.stack([128 * (p // 32) + 32 * be + (p % 32) for be in range(4)])


def _prep_weights(norm_g, w_in, attn_sink, dw_w, dw_b, conv_ln_g, conv_ln_b, w_out, final_g):
    w_in = np.asarray(w_in, np.float32)[0]
    w_out = np.asarray(w_out, np.float32)[0]
    perm = _perm_bank()
    cols = np.arange(PW)
    newq = np.empty(512, np.int64)
    for a in range(4):
        for g in range(2):
            newq[a * 128 + g * 64:a * 128 + g * 64 + 64] = (4 * g + a) * 64 + np.arange(64)
    cols[0:512] = newq
    cols[2304:2816] = 2304 + perm.reshape(-1)
    w_in_l = np.ascontiguousarray(w_in[:, cols].reshape(KC, 128, PW).transpose(1, 0, 2))
    rows = np.arange(D)
    rows[512:1024] = 512 + perm.reshape(-1)
    w_out_l = np.ascontiguousarray(w_out[rows, :].reshape(KC, 128, D).transpose(1, 0, 2))
    ng = np.ascontiguousarray(np.asarray(norm_g, np.float32)[0].reshape(KC, 128).T)
    dww = np.ascontiguousarray(np.asarray(dw_w, np.float32)[0].reshape(31, 4, 128).transpose(2, 1, 0))
    cvec = np.zeros((128, 20), np.float32)
    cvec[:, 0:4] = np.asarray(dw_b, np.float32)[0][perm].T
    cvec[:, 4:8] = np.asarray(conv_ln_g, np.float32)[0][perm].T
    cvec[:, 8:12] = np.asarray(conv_ln_b, np.float32)[0][perm].T
    id32 = np.zeros((128, 32), np.float32)
    id32[np.arange(128), np.arange(128) % 32] = 0.5
    return {
        "w_in": w_in_l, "ng": ng, "w_out": w_out_l, "dww": dww, "cvec": cvec,
        "sink": np.ascontiguousarray(np.asarray(attn_sink, np.float32).reshape(1, 8)),
        "fg": np.ascontiguousarray(np.asarray(final_g, np.float32).reshape(1, D)),
        "ident": np.eye(128, dtype=np.float32), "id32": id32,
    }


def _core_inputs(core, x_prompt, x_sample):
    segs = []
    kb = np.zeros((128, 2 * NTS), np.float32)
    buf = np.zeros((SEGT, D), np.float32)
    buf[128:128 + 4096] = x_prompt[core]
    kb[:, 0] = NEG
    kb[:, NTS - 1] = NEG
    segs.append(buf)
    b, q = core // 4, core % 4
    lo = q * 4096
    buf = np.zeros((SEGT, D), np.float32)
    a0, a1 = lo - 128, lo + 4096 + 128
    s0, s1 = max(a0, 0), min(a1, x_sample.shape[1])
    buf[s0 - a0:s1 - a0] = x_sample[b, s0:s1]
    if a0 < 0:
        kb[:, NTS] = NEG
    if a1 > x_sample.shape[1]:
        kb[:, 2 * NTS - 1] = NEG
    segs.append(buf)
    xT = np.stack([np.ascontiguousarray(s.T.reshape(KC, 128, SEGT).transpose(1, 0, 2)) for s in segs])
    xtok = np.stack([segs[0][128:128 + 4096], segs[1][128:128 + 4096]])
    return {"xT": xT, "xtok": np.ascontiguousarray(xtok), "kbias": kb}


_NC_CACHE = {}


def kernel(x_prompt, x_sample, norm_g, w_in, attn_sink, dw_w, dw_b, conv_ln_g, conv_ln_b, w_out, final_g):
    x_prompt = np.asarray(x_prompt, np.float32)
    x_sample = np.asarray(x_sample, np.float32)
    wts = _prep_weights(norm_g, w_in, attn_sink, dw_w, dw_b, conv_ln_g, conv_ln_b, w_out, final_g)
    in_maps = []
    for core in range(8):
        m = _core_inputs(core, x_prompt, x_sample)
        m.update(wts)
        in_maps.append(m)
    if "nc" not in _NC_CACHE:
        _NC_CACHE["nc"] = build_nc()
    res = run_bass_kernel_spmd(_NC_CACHE["nc"], in_maps, core_ids=list(range(8)))
    y_prompt = np.empty_like(x_prompt)
    y_sample = np.empty_like(x_sample)
    for core in range(8):
        y = res.results[core]["y"]
        y_prompt[core] = y[0]
        b, q = core // 4, core % 4
        y_sample[b, q * 4096:(q + 1) * 4096] = y[1]
    return (y_prompt, y_sample)
```

```python
from contextlib import ExitStack

import numpy as np
import concourse.bass as bass
import concourse.mybir as mybir
from concourse.alu_op_type import AluOpType as ALU
from concourse.bass_utils import run_bass_kernel_spmd

F32 = mybir.dt.float32
BF16 = mybir.dt.bfloat16
I32 = mybir.dt.int32
AF = mybir.ActivationFunctionType

D = 1024
KC = 8
PW = 2816
NTS = 34
SEGT = NTS * 128
EPS = 1e-6
NEG = -1e30
ENGS = ("pe", "act", "dve", "pool", "sp")


class Sched:
    def __init__(self, nc, stack):
        self.nc = nc
        self.stack = stack
        self.prog = {e: [] for e in ENGS}
        self.sems = {}
        self.cnt = {}
        self.seen = {e: {} for e in ENGS}
        self.lastw = {}
        self.readers = {}
        for e in ("pe", "act", "dve", "pool"):
            self._sem(e)
        self.n_wait = {e: 0 for e in ENGS}
        self.n_inst = {e: 0 for e in ENGS}

    def _sem(self, key):
        if key not in self.sems:
            name = "s_" + "_".join(str(k) for k in (key if isinstance(key, tuple) else (key,)))
            self.sems[key] = self.stack.enter_context(self.nc.semaphore(name))
            self.cnt[key] = 0
        return self.sems[key]

    def _deps(self, eng, reads, writes):
        deps = {}

        def add(ev):
            if ev is None:
                return
            k, v = ev
            if deps.get(k, 0) < v:
                deps[k] = v

        for r in reads:
            excl = isinstance(r, tuple) and r[0] == "ps"
            add(self.lastw.get(r))
            if excl:
                for ev in self.readers.get(r, ()):
                    if ev[0] != eng:
                        add(ev)
        for w in writes:
            ev = self.lastw.get(w)
            if ev is not None and ev[0] != eng:
                add(ev)
            for ev in self.readers.get(w, ()):
                if ev[0] != eng:
                    add(ev)
        out = []
        for k, v in deps.items():
            if k == "pe" and eng == "pe":
                continue
            if self.seen[eng].get(k, 0) >= v:
                continue
            self.seen[eng][k] = v
            out.append((k, v))
        return out

    def _record(self, ev, reads, writes):
        for w in writes:
            self.lastw[w] = ev
            self.readers[w] = []
        for r in reads:
            if r in writes:
                continue
            lst = self.readers.setdefault(r, [])
            lst[:] = [e for e in lst if e[0] != ev[0]]
            lst.append(ev)

    def _emit_waits(self, eng, deps):
        for k, v in deps:
            sem = self.sems[k]
            self.prog[eng].append(I("wait_ge", sem, v))
            self.n_wait[eng] += 1

    def op(self, eng, fn, reads=(), writes=()):
        deps = self._deps(eng, reads, writes)
        self._emit_waits(eng, deps)
        self.cnt[eng] += 1
        v = self.cnt[eng]
        sem = self.sems[eng]
        self.prog[eng].append(lambda h, fn=fn, sem=sem: fn(h).then_inc(sem, 1))
        self.n_inst[eng] += 1
        self._record((eng, v), reads, writes)
        return (eng, v)

    def group(self, eng, fns, reads=(), writes=()):
        deps = self._deps(eng, reads, writes)
        self._emit_waits(eng, deps)
        for fn in fns[:-1]:
            self.prog[eng].append(lambda h, fn=fn: fn(h))
        self.cnt[eng] += 1
        v = self.cnt[eng]
        sem = self.sems[eng]
        last = fns[-1]
        self.prog[eng].append(lambda h, fn=last, sem=sem: fn(h).then_inc(sem, 1))
        self.n_inst[eng] += len(fns)
        self._record((eng, v), reads, writes)
        return (eng, v)

    def dma(self, q, semkey, fn, reads=(), writes=()):
        sem = self._sem(semkey)
        deps = self._deps(q, reads, writes)
        self._emit_waits(q, deps)
        self.cnt[semkey] += 16
        v = self.cnt[semkey]
        self.prog[q].append(lambda h, fn=fn, sem=sem: fn(h).then_inc(sem, 16))
        self.n_inst[q] += 1
        self._record((semkey, v), reads, writes)
        return (semkey, v)

    def wait_all(self, eng, events):
        deps = []
        for k, v in events:
            if self.seen[eng].get(k, 0) < v:
                self.seen[eng][k] = v
                deps.append((k, v))
        self._emit_waits(eng, deps)

    def finish(self):
        nc = self.nc
        with nc.Block() as block:
            @block.sync
            def _(h):
                for f in self.prog["sp"]:
                    f(h)

            @block.tensor
            def _(h):
                for f in self.prog["pe"]:
                    f(h)

            @block.scalar
            def _(h):
                for f in self.prog["act"]:
                    f(h)

            @block.vector
            def _(h):
                for f in self.prog["dve"]:
                    f(h)

            @block.gpsimd
            def _(h):
                for f in self.prog["pool"]:
                    f(h)


def I(method, *a, **k):
    return lambda h: getattr(h, method)(*a, **k)


def build_nc(n_seg=2, n_st=8):
    nc = bass.Bass("TRN2", target_bir_lowering=False)

    def din(name, shape, dt=F32):
        return nc.dram_tensor(name, list(shape), dt, kind="ExternalInput").ap()

    xT_d = din("xT", [2, 128, KC, SEGT])
    xtok_d = din("xtok", [2, 4096, D])
    kbias_d = din("kbias", [128, 2 * NTS])
    win_d = din("w_in", [128, KC, PW])
    ng_d = din("ng", [128, KC])
    wout_d = din("w_out", [128, KC, D])
    dww_d = din("dww", [128, 4, 31])
    cvec_d = din("cvec", [128, 20])
    sink_d = din("sink", [1, 8])
    fg_d = din("fg", [1, D])
    ident_d = din("ident", [128, 128])
    id32_d = din("id32", [128, 32])
    y_d = nc.dram_tensor("y", [2, 4096, D], F32, kind="ExternalOutput").ap()

    with ExitStack() as st:
        S = Sched(nc, st)

        def sb(name, shape, dt):
            return st.enter_context(nc.sbuf_tensor("sb_" + name, list(shape), dt))

        Wp = sb("Wp", [128, KC, PW], BF16)
        Wo = sb("Wo", [128, KC, D], BF16)
        DG = sb("DG", [128, 4, 31, 32], BF16)
        ET = sb("ET", [128, 3, 2, 512], BF16)
        identb = sb("identb", [128, 128], BF16)
        onesb = sb("onesb", [128, 128], BF16)
        fgb = sb("fgb", [128, D], F32)
        ng = sb("ng", [128, KC], F32)
        dww = sb("dww", [128, 4, 31], F32)
        cvec = sb("cvec", [128, 20], F32)
        esink = sb("esink", [128, 8], F32)
        kbias = sb("kbias", [128, 2 * NTS], F32)
        nhalf = sb("nhalf", [128, 1], F32)
        id32 = sb("id32", [128, 32], F32)

        XT = sb("XT", [128, KC, 512], F32)
        XN = sb("XN", [128, KC, 512], BF16)
        XSQ = sb("XSQ", [128, KC, 512], BF16)
        VBC = sb("VBC", [128, 512], F32)
        RBC = sb("RBC", [128, 512], F32)
        QT = sb("QT", [128, 4, 640], BF16)
        KT = sb("KT", [128, 8, 128], BF16)
        VP = sb("VP", [128, 8, 2, 65], BF16)
        HB = sb("HB", [128, 4, 784], BF16)
        SGC = sb("SGC", [128, 4, 640], BF16)
        SGA = sb("SGA", [128, 8, 512], BF16)
        TH = sb("TH", [128, 2, 512], F32)
        TH2 = sb("TH2", [128, 2, 512], F32)
        PT = sb("PT", [128, 2, 3, 512], BF16)
        MA = sb("MA", [128, 2, 512], BF16)
        TO = sb("TO", [128, 2, 256], F32)
        DEN = sb("DEN", [128, 2, 8], F32)
        MT = sb("MT", [128, KC, 512], BF16)
        YB = sb("YB", [128, 2, 512], BF16)
        YSQ = sb("YSQ", [128, 2, 512], BF16)
        MU = sb("MU", [128, 512], F32)
        VR = sb("VR", [128, 512], F32)
        T1 = sb("T1", [128, 2, 512], F32)
        LL = sb("LL", [128, 2, 512], F32)
        XTOK = sb("XTOK", [128, 3, D], F32)
        NWA = sb("NWA", [128, 512], F32)
        NWT = sb("NWT", [128, 512], I32)
        RR = sb("RR", [128, 512], F32)
        SS = sb("SS", [128, 4], F32)

        Df = TH2[:, 0, 0:128]
        Dabs = TH2[:, 0, 128:256]
        Mle = TH2[:, 0, 256:384]
        Mge = TH2[:, 0, 384:512]
        Etmp = TH2[:, 1, 0:128]
        identf = TH2[:, 1, 128:256]
        Di = TH2[:, 1, 256:384].bitcast(I32)
        JUNK = YB[:, :, :].rearrange("p a b -> p (a b)")
        PS = [st.enter_context(nc.psum_tensor("ps%d" % i, [128, 512], F32)) for i in range(8)]
        bank_ctr = [0]

        def nb():
            b = bank_ctr[0] % 8
            bank_ctr[0] += 1
            return b

        def pk(b):
            return ("ps", b)

        def ld(key, dst, src, wkey):
            S.dma("sp", key, I("dma_start", out=dst, in_=src), writes=[wkey])

        ld("c_ng", ng[:], ng_d, "ng")
        ld("c_dww", dww[:], dww_d, "dww")
        ld("c_cvec", cvec[:], cvec_d, "cvec")
        ld("c_kb", kbias[:], kbias_d, "kbias")
        ld("c_id", identf, ident_d, "identf")
        ld("c_id32", id32[:], id32_d, "id32")
        S.dma("sp", "c_sink", I("dma_start", out=esink[:], in_=sink_d.to_broadcast([128, 8])),
              writes=["esink"])
        S.dma("sp", "c_fg", I("dma_start", out=fgb[:], in_=fg_d.to_broadcast([128, D])),
              writes=["fgb"])

        S.op("pool", I("memset", nhalf[:], -0.5), writes=["nhalf"])
        S.op("pool", I("memset", onesb[:], 1.0), writes=["onesb"])
        S.op("pool", I("memset", VP[:, :, :, 64:65], 1.0), writes=[("VP", s) for s in range(8)])
        S.op("dve", I("tensor_copy", out=identb[:], in_=identf), reads=["identf"], writes=["identb"])
        S.op("act", I("activation", out=esink[:], in_=esink[:], func=AF.Exp),
             reads=["esink"], writes=["esink"])
        S.op("dve", I("tensor_scalar", out=cvec[:, 12:20], in0=cvec[:, 4:12], scalar1=0.5, scalar2=None,
                                              op0=ALU.mult), reads=["cvec"], writes=["cvec"])

        S.op("pool", I("iota", Di, pattern=[[1, 128]], base=0, channel_multiplier=-1), writes=["Di"])
        S.op("dve", I("tensor_copy", out=Df, in_=Di), reads=["Di"], writes=["Df"])
        S.op("act", I("activation", out=Dabs, in_=Df, func=AF.Abs), reads=["Df"], writes=["Dabs"])
        S.op("dve", I("tensor_scalar", out=Mle, in0=Df, scalar1=0.0, scalar2=None, op0=ALU.is_le),
             reads=["Df"], writes=["Mle"])
        S.op("dve", I("tensor_scalar", out=Mge, in0=Df, scalar1=0.0, scalar2=None, op0=ALU.is_ge),
             reads=["Df"], writes=["Mge"])
        for hd in range(8):
            m = 2.0 ** (-(hd + 1))
            g, a = hd // 4, hd % 4
            cs = slice(a * 128, (a + 1) * 128)
            S.op("act", I("activation", out=Etmp, in_=Df, func=AF.Exp, scale=-m, bias=-128.0 * m),
                 reads=["Df"], writes=["Etmp"])
            S.op("dve", I("tensor_tensor", out=ET[:, 0, g, cs], in0=Etmp, in1=Mle, op=ALU.mult),
                 reads=["Etmp", "Mle"], writes=["ET"])
            S.op("act", I("activation", out=ET[:, 1, g, cs], in_=Dabs, func=AF.Exp, scale=-m),
                 reads=["Dabs"], writes=["ET"])
            S.op("act", I("activation", out=Etmp, in_=Df, func=AF.Exp, scale=m, bias=-128.0 * m),
                 reads=["Df"], writes=["Etmp"])
            S.op("dve", I("tensor_tensor", out=ET[:, 2, g, cs], in0=Etmp, in1=Mge, op=ALU.mult),
                 reads=["Etmp", "Mge"], writes=["ET"])

        XTf = XT[:, 0:6, :].rearrange("p a b -> p (a b)")
        XKf = XTOK[:, :, :].rearrange("p a b -> p (a b)")
        for kc in range(KC):
            stg, skey, eng, dkey = (XTf, "XT", "dve", "w_st0") if kc % 2 == 0 else (XKf, "XTOKall", "pool", "w_st1")
            S.dma("sp", dkey, I("dma_start", out=stg[:, 0:PW], in_=win_d[:, kc, :]), writes=[skey])
            S.op(eng, I("tensor_scalar", out=Wp[:, kc, 0:512], in0=stg[:, 0:512], scalar1=ng[:, kc:kc + 1],
                        scalar2=0.125, op0=ALU.mult, op1=ALU.mult), reads=[skey, "ng"], writes=["Wp"])
            S.op(eng, I("tensor_scalar", out=Wp[:, kc, 512:PW], in0=stg[:, 512:PW], scalar1=ng[:, kc:kc + 1],
                        scalar2=1.0, op0=ALU.mult, op1=ALU.mult), reads=[skey, "ng"], writes=["Wp"])
        for hf in range(2):
            S.dma("sp", "w_st0", I("dma_start", out=XT[:, :, :].rearrange("p a b -> p (a b)").rearrange("p (k c) -> p k c", c=D),
                                   in_=wout_d[:, 4 * hf:4 * hf + 4, :]), writes=["XT"])
            S.op("dve", I("tensor_scalar", out=Wo[:, 4 * hf:4 * hf + 4, :],
                          in0=XT[:, :, :].rearrange("p a b -> p (a b)").rearrange("p (k c) -> p k c", c=D),
                          scalar1=(0.5 if hf == 0 else 0.25), scalar2=None, op0=ALU.mult), reads=["XT"], writes=["Wo"])
        for c in range(4):
            S.op("pool", I("tensor_tensor", out=DG[:, c, :, :], in0=id32[:, :].unsqueeze(1).to_broadcast([128, 31, 32]),
                           in1=dww[:, c, :].unsqueeze(2).to_broadcast([128, 31, 32]), op=ALU.mult),
                 reads=["id32", "dww"], writes=["DG"])
        def rsqrt_bc(R, rk, V, vk, N):
            S.op("dve", I("tensor_scalar", out=NWT[:, 0:N], in0=V.bitcast(I32), scalar1=1, scalar2=None,
                          op0=ALU.arith_shift_right), reads=[vk], writes=["NWT"])
            S.op("dve", I("tensor_scalar", out=R.bitcast(I32), in0=NWT[:, 0:N], scalar1=-1.0, scalar2=1597463007.0,
                          op0=ALU.mult, op1=ALU.add), reads=["NWT"], writes=[rk])
            for _ in range(2):
                S.op("dve", I("tensor_tensor", out=NWA[:, 0:N], in0=R, in1=R, op=ALU.mult), reads=[rk], writes=["NWA"])
                S.op("dve", I("scalar_tensor_tensor", out=NWA[:, 0:N], in0=NWA[:, 0:N], scalar=-0.5, in1=V,
                              op0=ALU.mult, op1=ALU.mult), reads=["NWA", vk], writes=["NWA"])
                S.op("dve", I("scalar_tensor_tensor", out=R, in0=NWA[:, 0:N], scalar=1.5, in1=R,
                              op0=ALU.add, op1=ALU.mult), reads=["NWA", rk], writes=[rk])

        def slot(tile):
            return (tile - 1) % 8

        pool = {"banks": list(range(8)), "i": 0}

        def nb():
            lst = pool["banks"]
            b = lst[pool["i"] % len(lst)]
            pool["i"] += 1
            return b

        def set_pool(lst):
            pool["banks"] = list(lst)
            pool["i"] = 0

        thc = [0]

        def th_slot():
            thc[0] += 1
            return thc[0] % 2

        xtc = [0]
        ssc = [0]
        stores = {}

        def prep_early(seg, tok0, T):
            def s_a():
                S.dma("sp", "xT", I("dma_start", out=XT[:, :, 0:T], in_=xT_d[seg, :, :, tok0:tok0 + T]),
                      writes=["XT"])
                S.op("act", I("activation", out=XSQ[:, :, 0:T], in_=XT[:, :, 0:T], func=AF.Square),
                     reads=["XT"], writes=["XSQ"])

            def s_b():
                b = nb()
                S.group("pe", [I("matmul", PS[b][:, 0:T], lhsT=onesb[:], rhs=XSQ[:, kc, 0:T],
                                 start=(kc == 0), stop=(kc == KC - 1)) for kc in range(KC)],
                        reads=["XSQ", "onesb"], writes=[pk(b)])
                S.op("act", I("activation", out=VBC[:, 0:T], in_=PS[b][:, 0:T], func=AF.Identity,
                              scale=1.0 / D, bias=EPS), reads=[pk(b)], writes=["VBC"])

            def s_c():
                rsqrt_bc(RBC[:, 0:T], "RBC", VBC[:, 0:T], "VBC", T)
            return [s_a, s_b, s_c]

        def prep_xn(T):
            def s_x(k0):
                def f():
                    for kc in range(k0, k0 + 2):
                        S.op("dve", I("tensor_tensor", out=XN[:, kc, 0:T], in0=XT[:, kc, 0:T], in1=RBC[:, 0:T],
                                      op=ALU.mult), reads=["XT", "RBC"], writes=["XN"])
                return f
            return [s_x(k0) for k0 in range(0, KC, 2)]

        def fm_chunk(c0, T):
            b = nb()
            S.group("pe", [I("matmul", PS[b][:, 0:T], lhsT=Wp[:, kc, c0:c0 + 128], rhs=XN[:, kc, 0:T],
                             start=(kc == 0), stop=(kc == KC - 1)) for kc in range(KC)],
                    reads=["XN", "Wp"], writes=[pk(b)])
            return b

        def a_steps(seg, tile0, nt, full, qcol, hcol, shift=True):
            T = nt * 128
            main, gcs = [], []

            def s_shift():
                S.op("act", I("activation", out=QT[:, :, 0:128], in_=QT[:, :, 512:640], func=AF.Copy),
                     reads=["QT"], writes=["QT"])
                S.op("act", I("activation", out=HB[:, :, 0:144], in_=HB[:, :, 512:656], func=AF.Copy),
                     reads=["HB"], writes=["HB"])
            if full and shift:
                main.append(s_shift)

            def s_q(a):
                def f():
                    b = fm_chunk(a * 128, T)
                    S.op("act", I("activation", out=QT[:, a, qcol:qcol + T], in_=PS[b][:, 0:T], func=AF.Copy),
                         reads=[pk(b)], writes=["QT"])
                return f

            def s_k():
                b = fm_chunk(512, T)
                for i in range(nt):
                    s = slot(tile0 + i)
                    S.op("act", I("activation", out=KT[:, s, :], in_=PS[b][:, i * 128:(i + 1) * 128], func=AF.Copy),
                         reads=[pk(b)], writes=[("KT", s)])

            def s_glu(c):
                def f():
                    bv = fm_chunk(1280 + c * 128, T)
                    bg = fm_chunk(1792 + c * 128, T)
                    ts = th_slot()
                    S.op("act", I("activation", out=TH[:, ts, 0:T], in_=PS[bg][:, 0:T], func=AF.Tanh, scale=0.5),
                         reads=[pk(bg)], writes=[("TH", ts)])
                    S.op("dve", I("scalar_tensor_tensor", out=HB[:, c, hcol:hcol + T], in0=TH[:, ts, 0:T], scalar=1.0,
                                  in1=PS[bv][:, 0:T], op0=ALU.add, op1=ALU.mult),
                         reads=[("TH", ts), pk(bv)], writes=["HB"])
                return f

            def s_gc(c):
                def f():
                    b = fm_chunk(2304 + c * 128, T)
                    ts = th_slot()
                    S.op("act", I("activation", out=TH[:, ts, 0:T], in_=PS[b][:, 0:T], func=AF.Tanh, scale=0.5),
                         reads=[pk(b)], writes=[("TH", ts)])
                    S.op("dve", I("scalar_tensor_tensor", out=SGC[:, c, qcol:qcol + T], in0=TH[:, ts, 0:T], scalar=1.0,
                                  in1=PS[b][:, 0:T], op0=ALU.add, op1=ALU.mult),
                         reads=[("TH", ts), pk(b)], writes=["SGC"])
                return f

            def s_v():
                bV = nb()
                for i in range(nt):
                    S.group("pe", [I("matmul", PS[bV][:, i * 128:(i + 1) * 128], lhsT=XN[:, kc, i * 128:(i + 1) * 128],
                                     rhs=Wp[:, kc, 640:768], start=(kc == 0), stop=(kc == KC - 1))
                                   for kc in range(KC)], reads=["XN", "Wp"], writes=[pk(bV)])
                for i in range(nt):
                    s = slot(tile0 + i)
                    S.op("act", I("activation", out=VP[:, s, :, 0:64],
                                  in_=PS[bV][:, i * 128:(i + 1) * 128].rearrange("p (g d) -> p g d", g=2),
                                  func=AF.Copy), reads=[pk(bV)], writes=[("VP", s)])

            def s_ga(i):
                def f():
                    s = slot(tile0 + i)
                    b = nb()
                    S.group("pe", [I("matmul", PS[b][:, :], lhsT=XN[:, kc, i * 128:(i + 1) * 128],
                                     rhs=Wp[:, kc, 768:1280], start=(kc == 0), stop=(kc == KC - 1))
                                   for kc in range(KC)], reads=["XN", "Wp"], writes=[pk(b)])
                    ts = th_slot()
                    S.op("act", I("activation", out=TH[:, ts, :], in_=PS[b][:, :], func=AF.Tanh, scale=0.5),
                         reads=[pk(b)], writes=[("TH", ts)])
                    S.op("dve", I("scalar_tensor_tensor", out=SGA[:, s, :], in0=TH[:, ts, :], scalar=1.0,
                                  in1=PS[b][:, :], op0=ALU.add, op1=ALU.mult),
                         reads=[("TH", ts), pk(b)], writes=[("SGA", s)])
                return f

            main.append(s_k)
            main.append(s_v)
            if full:
                for a in range(4):
                    main.append(s_q(a))
                for i in range(nt):
                    main.append(s_ga(i))
            for c in range(4):
                f_ = s_glu(c)
                f_._w = 3.4
                main.append(f_)

            def s_shift_gc():
                S.op("act", I("activation", out=SGC[:, :, 0:128], in_=SGC[:, :, 512:640], func=AF.Copy),
                     reads=["SGC"], writes=["SGC"])
            if full:
                if shift:
                    gcs.append(s_shift_gc)
                for c in range(4):
                    gcs.append(s_gc(c))
            return main, gcs

        BS = [4, 5, 6]
        BO = 7

        def att_steps(seg, qt0, nq, qcol):
            steps = []

            def half_a(i, g):
                def f():
                    qt = qt0 + i
                    qc = qcol + i * 128
                    rows = slice(64 * g, 64 * g + 64)
                    for kb in range(3):
                        ks = slot(qt - 1 + kb)
                        S.op("pe", I("matmul", PS[BS[kb]][:, :], lhsT=KT[rows, ks, :], rhs=QT[rows, :, qc:qc + 128],
                                     start=True, stop=True), reads=[("KT", ks), "QT"], writes=[pk(BS[kb])])
                    for kb in range(3):
                        kcol = seg * NTS + qt - 1 + kb
                        S.op("act", I("activation", out=PT[:, g, kb, :], in_=PS[BS[kb]][:, :], func=AF.Exp,
                                      bias=kbias[:, kcol:kcol + 1]), reads=[pk(BS[kb]), "kbias"], writes=[("PT", g, kb)])
                    S.op("dve", I("tensor_tensor", out=PT[:, g, :, :], in0=PT[:, g, :, :], in1=ET[:, :, g, :],
                                  op=ALU.mult), reads=[("PT", g, kb) for kb in range(3)] + ["ET"],
                         writes=[("PT", g, kb) for kb in range(3)])
                return f

            def half_b(i, g):
                def f():
                    qt = qt0 + i
                    ms = i % 2
                    fns = []
                    for a in range(4):
                        for kb in range(3):
                            ks = slot(qt - 1 + kb)
                            fns.append(I("matmul", PS[BO][:, a * 65:(a + 1) * 65], lhsT=PT[:, g, kb, a * 128:(a + 1) * 128],
                                         rhs=VP[:, ks, g, :], start=(kb == 0), stop=(kb == 2)))
                    S.group("pe", fns, reads=[("PT", g, kb) for kb in range(3)] +
                            [("VP", slot(qt - 1 + kb)) for kb in range(3)], writes=[pk(BO)])
                    Ov = PS[BO][:, 0:260].rearrange("p (a e) -> p a e", a=4)
                    S.op("dve", I("tensor_tensor", out=DEN[:, g, 0:4], in0=Ov[:, :, 64], in1=esink[:, 4 * g:4 * g + 4],
                                  op=ALU.add), reads=[pk(BO), "esink"], writes=[("DEN", g)])
                    S.op("dve", I("reciprocal", out=DEN[:, g, 4:8], in_=DEN[:, g, 0:4]),
                         reads=[("DEN", g)], writes=[("DEN", g)])
                    S.op("dve", I("tensor_tensor", out=TO[:, g, :].rearrange("p (a d) -> p a d", a=4), in0=Ov[:, :, 0:64],
                                  in1=DEN[:, g, 4:8].unsqueeze(2).to_broadcast([128, 4, 64]), op=ALU.mult),
                         reads=[pk(BO), ("DEN", g)], writes=[("TO", g)])
                    S.op("dve", I("tensor_tensor", out=MA[:, ms, 256 * g:256 * g + 256], in0=TO[:, g, :],
                                  in1=SGA[:, slot(qt), 256 * g:256 * g + 256], op=ALU.mult),
                         reads=[("TO", g), ("SGA", slot(qt))], writes=[("MA", ms, g)])
                return f

            def half_c(i):
                def f():
                    ms = i % 2
                    Tv = PS[BO][:].bitcast(BF16)
                    S.group("pe", [I("transpose", Tv[:, c * 128:(c + 1) * 128], MA[:, ms, c * 128:(c + 1) * 128],
                                     identb[:]) for c in range(4)],
                            reads=[("MA", ms, 0), ("MA", ms, 1), "identb"], writes=[pk(BO)])
                    S.op("act", I("activation", out=MT[:, 0:4, i * 128:(i + 1) * 128],
                                  in_=Tv[:, 0:512].rearrange("p (c q) -> p c q", c=4), func=AF.Copy),
                         reads=[pk(BO)], writes=[("MTa", i)])
                return f
            units = [(i, g) for i in range(nq) for g in range(2)]
            n = len(units)
            for u, (i, g) in enumerate(units):
                steps.append(half_a(i, g))
                if u >= 3 and u % 2 == 1:
                    steps.append(half_c((u - 3) // 2))
                if u >= 1:
                    steps.append(half_b(*units[u - 1]))
            steps.append(half_b(*units[-1]))
            if nq >= 2:
                pass
            done = set((u - 3) // 2 for u in range(n) if u >= 3 and u % 2 == 1)
            for i in range(nq):
                if i not in done:
                    steps.append(half_c(i))
            return steps

        BY = [0, 1, 2, 3]

        def conv_steps(hcol, N, npieces=8):
            taps = [(c, j) for j in range(31) for c in range(4)]
            per = (len(taps) + npieces - 1) // npieces
            steps = []

            def piece(lst):
                def f():
                    fns = []
                    for c, j in lst:
                        for q4 in range(4):
                            r = slice(32 * q4, 32 * q4 + 32)
                            fns.append(I("matmul", PS[BY[q4]][32 * c:32 * c + 32, 0:N], lhsT=DG[r, c, j, :],
                                         rhs=HB[r, c, hcol + j - 15:hcol + j - 15 + N],
                                         start=(j == 0), stop=(j == 30), tile_position=(32 * q4, 32 * c)))
                    S.group("pe", fns, reads=["HB", "DG"], writes=[pk(b) for b in BY])
                return f
            for k in range(0, len(taps), per):
                steps.append(piece(taps[k:k + per]))
            return steps

        def ln_steps(N, qcol):
            b1, b2 = 4, 5
            steps = []

            def stat_a(be):
                def f():
                    sl = be % 2
                    S.op("act", I("activation", out=YB[:, sl, 0:N], in_=PS[BY[be]][:, 0:N], func=AF.Identity,
                                  bias=cvec[:, be:be + 1]), reads=[pk(BY[be]), "cvec"], writes=[("YB", sl)])
                    S.op("act", I("activation", out=YSQ[:, sl, 0:N], in_=PS[BY[be]][:, 0:N], func=AF.Square,
                                  bias=cvec[:, be:be + 1]), reads=[pk(BY[be]), "cvec"], writes=[("YSQ", sl)])
                return f

            def stat_b(be):
                def f():
                    sl = be % 2
                    S.op("pe", I("matmul", PS[b1][:, 0:N], lhsT=onesb[:], rhs=YB[:, sl, 0:N], start=(be == 0),
                                 stop=(be == 3)), reads=[("YB", sl), "onesb"], writes=[pk(b1)])
                    S.op("pe", I("matmul", PS[b2][:, 0:N], lhsT=onesb[:], rhs=YSQ[:, sl, 0:N], start=(be == 0),
                                 stop=(be == 3)), reads=[("YSQ", sl), "onesb"], writes=[pk(b2)])
                return f

            def fin_a():
                S.op("act", I("activation", out=MU[:, 0:N], in_=PS[b1][:, 0:N], func=AF.Identity, scale=1.0 / 512),
                     reads=[pk(b1)], writes=["MU"])
                S.op("act", I("activation", out=VR[:, 0:N], in_=PS[b2][:, 0:N], func=AF.Identity, scale=1.0 / 512,
                              bias=EPS), reads=[pk(b2)], writes=["VR"])
                pool["banks"] = pool["banks"] + [b1, b2]

            def fin_d():
                S.op("dve", I("tensor_tensor", out=NWA[:, 0:N], in0=MU[:, 0:N], in1=MU[:, 0:N], op=ALU.mult),
                     reads=["MU"], writes=["NWA"])
                S.op("dve", I("tensor_tensor", out=VR[:, 0:N], in0=VR[:, 0:N], in1=NWA[:, 0:N], op=ALU.subtract),
                     reads=["VR", "NWA"], writes=["VR"])
                rsqrt_bc(RR[:, 0:N], "RR", VR[:, 0:N], "VR", N)

            tsl = {}

            def t1_d(be):
                def f():
                    sl = be % 2
                    S.op("dve", I("scalar_tensor_tensor", out=T1[:, sl, 0:N], in0=PS[BY[be]][:, 0:N],
                                  scalar=cvec[:, be:be + 1], in1=MU[:, 0:N], op0=ALU.add, op1=ALU.subtract),
                         reads=[pk(BY[be]), "cvec", "MU"], writes=[("T1", sl)])
                    S.op("dve", I("tensor_tensor", out=T1[:, sl, 0:N], in0=T1[:, sl, 0:N], in1=RR[:, 0:N], op=ALU.mult),
                         reads=[("T1", sl), "RR"], writes=[("T1", sl)])
                return f

            def t1_a(be):
                def f():
                    sl = be % 2
                    ts = be % 2
                    tsl[be] = ts
                    S.op("act", I("activation", out=LL[:, sl, 0:N], in_=T1[:, sl, 0:N], func=AF.Identity,
                                  scale=cvec[:, 4 + be:5 + be], bias=cvec[:, 8 + be:9 + be]),
                         reads=[("T1", sl), "cvec"], writes=[("LL", sl)])
                    S.op("act", I("activation", out=TH2[:, ts, 0:N], in_=T1[:, sl, 0:N], func=AF.Tanh,
                                  scale=cvec[:, 12 + be:13 + be], bias=cvec[:, 16 + be:17 + be]),
                         reads=[("T1", sl), "cvec"], writes=[("TH2", ts)])
                return f

            def t1_z(be):
                def f():
                    sl = be % 2
                    ts = tsl[be]
                    S.op("dve", I("scalar_tensor_tensor", out=LL[:, sl, 0:N], in0=TH2[:, ts, 0:N], scalar=1.0,
                                  in1=LL[:, sl, 0:N], op0=ALU.add, op1=ALU.mult),
                         reads=[("TH2", ts), ("LL", sl)], writes=[("LL", sl)])
                    S.op("dve", I("tensor_tensor", out=MT[:, 4 + be, 0:N], in0=LL[:, sl, 0:N],
                                  in1=SGC[:, be, qcol:qcol + N], op=ALU.mult),
                         reads=[("LL", sl), "SGC"], writes=[("MTc", be)])
                return f
            steps += [stat_a(0), stat_a(1), stat_b(0), stat_a(2), stat_b(1), stat_a(3), stat_b(2), stat_b(3)]
            steps += [fin_a, fin_d]
            steps += [t1_d(0), t1_d(1), t1_a(0), t1_d(2), t1_a(1), t1_z(0), t1_d(3), t1_a(2), t1_z(1), t1_a(3),
                      t1_z(2), t1_z(3)]
            return steps

        def outp_steps(seg, qt0, nq):
            tokbase = (qt0 - 1) * 128
            xsl = {}

            def load_x(i):
                sl = xtc[0] % 3
                xtc[0] += 1
                xsl[i] = sl
                t0 = tokbase + i * 128
                S.dma("sp", ("xl", sl), I("dma_start", out=XTOK[:, sl, :], in_=xtok_d[seg, t0:t0 + 128, :]),
                      writes=[("XTOK", sl), "XTOKall"] if xtc[0] <= 3 else [("XTOK", sl)])

            def tile(i):
                def f():
                    if i == 0:
                        load_x(0)
                    if i + 1 < nq:
                        load_x(i + 1)
                    sl = xsl[i]
                    t0 = tokbase + i * 128
                    bc = [nb(), nb()]
                    for hf in range(2):
                        S.group("pe", [I("matmul", PS[bc[hf]][:, :], lhsT=MT[:, kc, i * 128:(i + 1) * 128],
                                         rhs=Wo[:, kc, hf * 512:(hf + 1) * 512], start=(kc == 0), stop=(kc == KC - 1))
                                       for kc in range(KC)],
                                reads=[("MTa", i)] + [("MTc", be) for be in range(4)] + ["Wo"], writes=[pk(bc[hf])])
                    for hf in range(2):
                        S.op("dve", I("tensor_tensor", out=XTOK[:, sl, hf * 512:(hf + 1) * 512], in0=PS[bc[hf]][:, :],
                                      in1=XTOK[:, sl, hf * 512:(hf + 1) * 512], op=ALU.add),
                             reads=[pk(bc[hf]), ("XTOK", sl)], writes=[("XTOK", sl)])
                    sc = ssc[0] % 4
                    ssc[0] += 1
                    S.op("act", I("activation", out=JUNK, in_=XTOK[:, sl, :], func=AF.Square,
                                  accum_out=SS[:, sc:sc + 1]), reads=[("XTOK", sl)], writes=[("YB", 0), ("YB", 1), ("SS", sc)])
                    S.op("dve", I("tensor_scalar", out=SS[:, sc:sc + 1], in0=SS[:, sc:sc + 1], scalar1=1.0 / D,
                                  scalar2=EPS, op0=ALU.mult, op1=ALU.add), reads=[("SS", sc)], writes=[("SS", sc)])
                    S.op("pool", I("tensor_tensor", out=SS[:, sc:sc + 1], in0=SS[:, sc:sc + 1], in1=nhalf[:, 0:1],
                                   op=ALU.pow), reads=[("SS", sc), "nhalf"], writes=[("SS", sc)])
                    S.op("dve", I("scalar_tensor_tensor", out=XTOK[:, sl, :], in0=XTOK[:, sl, :], scalar=SS[:, sc:sc + 1],
                                  in1=fgb[:], op0=ALU.mult, op1=ALU.mult),
                         reads=[("XTOK", sl), ("SS", sc), "fgb"], writes=[("XTOK", sl)])
                    ev = S.dma("sp", ("st", sl), I("dma_start", out=y_d[seg, t0:t0 + 128, :], in_=XTOK[:, sl, :]),
                               reads=[("XTOK", sl)], writes=[])
                    stores[("st", sl)] = ev
                return f
            return [tile(i) for i in range(nq)]

        def run(steps):
            for f in steps:
                f()

        def merge_w(la, wa, lb, wb):
            if not la:
                return list(lb)
            if not lb:
                return list(la)
            def mids(w):
                tot = float(sum(w))
                out, c = [], 0.0
                for x in w:
                    out.append((c + 0.5 * x) / tot)
                    c += x
                return out
            ma, mb = mids(wa), mids(wb)
            out, i, k = [], 0, 0
            while i < len(la) or k < len(lb):
                if k >= len(lb) or (i < len(la) and ma[i] <= mb[k]):
                    out.append(la[i]); i += 1
                else:
                    out.append(lb[k]); k += 1
            return out

        def merge(*lists):
            lists = [l for l in lists if l]
            out = []
            pos = [0] * len(lists)
            total = sum(len(l) for l in lists)
            for _ in range(total):
                best, bi = None, None
                for k, l in enumerate(lists):
                    if pos[k] < len(l):
                        frac = (pos[k] + 0.5) / len(l)
                        if best is None or frac < best:
                            best, bi = frac, k
                out.append(lists[bi][pos[bi]])
                pos[bi] += 1
            return out

        def unit_prep(u):
            kind, sg, jj = u
            if kind == "st":
                return prep_early(sg, 128 + 512 * jj, 512), 512
            if kind == "haloR":
                return prep_early(sg, 128 + 512 * n_st, 128), 128
            if kind == "haloL":
                return prep_early(sg, 0, 128), 128
            return prep_early(sg, 128, 512), 512

        def unit_a(u):
            kind, sg, jj = u
            if kind == "st":
                return a_steps(sg, 4 * jj + 1, 4, True, 128, 144)
            if kind == "haloR":
                return a_steps(sg, 4 * n_st + 1, 1, False, None, 656)
            if kind == "haloL":
                return a_steps(sg, 0, 1, False, None, 16)
            raise AssertionError(kind)

        t_pending = None
        for seg in range(n_seg):
            last = 4 * n_st
            set_pool(range(8))
            if seg == 0:
                run(prep_early(seg, 0, 128) + prep_xn(128))
                m, _ = a_steps(seg, 0, 1, False, None, 16)
                run(merge(m, prep_early(seg, 128, 512)))
            run(prep_xn(512))
            m, g = a_steps(seg, 1, 4, True, 128, 144, shift=False)
            units = [("st", seg, jj) for jj in range(1, n_st)] + [("haloR", seg, None)]
            if seg + 1 < n_seg:
                units += [("haloL", seg + 1, None), ("st0", seg + 1, None)]
            pe0, t_pending = unit_prep(units[0])
            run(merge(m + g, pe0))
            for j in range(n_st + 1):
                if j == 0:
                    win = (1, 3, 128, 144)
                elif j < n_st:
                    win = (4 * j, 4, 0, 16)
                else:
                    win = (last, 1, 512, 528)
                qt0, nq, qcol, hcol = win
                N = nq * 128
                nxt = units[j] if j < len(units) else None
                nxt2 = units[j + 1] if j + 1 < len(units) else None
                if nxt is not None:
                    nxt_xn = prep_xn(t_pending)
                    nxt_a, nxt_g = unit_a(nxt)
                else:
                    nxt_xn, nxt_a, nxt_g = [], [], []
                if nxt2 is not None:
                    nxt2_prep, t_pending = unit_prep(nxt2)
                else:
                    nxt2_prep = []
                att = att_steps(seg, qt0, nq, qcol)
                ntail = min(len(att), 2 + (1 if nq >= 2 else 0))
                att_head, att_tail = att[:len(att) - ntail], att[len(att) - ntail:]
                p1 = merge(att_head, conv_steps(hcol, N))
                for r, stp in enumerate(nxt_xn):
                    p1.insert(min(len(p1), 1 + 2 * r), stp)
                run(p1)
                set_pool([6, 7])
                if nxt is not None and nxt[0] == "haloL":
                    run(att_tail)
                    att_tail = []
                lnl = ln_steps(N, qcol)
                w_ln = [1.5] * max(0, len(att_tail) - 1) + [1.0] * min(1, len(att_tail)) + \
                       [1.2, 1.2, 0.4, 1.2, 0.4, 1.2, 0.4, 0.4, 1.2, 7.0] + [1.2] * 12
                assert len(w_ln) == len(att_tail) + len(lnl)
                w_a = []
                for stp in nxt_a:
                    w_a.append(getattr(stp, "_w", 1.7))
                w_a += [0.3, 1.7, 0.3][:len(nxt2_prep)]
                run(merge_w(att_tail + lnl + [lambda: None], w_ln + [8.0], nxt_a + nxt2_prep, w_a))
                set_pool([4, 5, 6, 7])
                run(merge(outp_steps(seg, qt0, nq), nxt_g))

        S.wait_all("sp", list(stores.values()))
        S.finish()
        nc._sched_stats = (S.n_inst, S.n_wait)
    return nc


def _perm_bank():
    p = np.arange(128)
    return np.stack([128 * (p // 32) + 32 * be + (p % 32) for be in range(4)])


def _prep_weights(norm_g, w_in, attn_sink, dw_w, dw_b, conv_ln_g, conv_ln_b, w_out, final_g):
    w_in = np.asarray(w_in, np.float32)[0]
    w_out = np.asarray(w_out, np.float32)[0]
    perm = _perm_bank()
    cols = np.arange(PW)
    newq = np.empty(512, np.int64)
    for a in range(4):
        for g in range(2):
            newq[a * 128 + g * 64:a * 128 + g * 64 + 64] = (4 * g + a) * 64 + np.arange(64)
    cols[0:512] = newq
    cols[2304:2816] = 2304 + perm.reshape(-1)
    w_in_l = np.ascontiguousarray(w_in[:, cols].reshape(KC, 128, PW).transpose(1, 0, 2))
    rows = np.arange(D)
    rows[512:1024] = 512 + perm.reshape(-1)
    w_out_l = np.ascontiguousarray(w_out[rows, :].reshape(KC, 128, D).transpose(1, 0, 2))
    ng = np.ascontiguousarray(np.asarray(norm_g, np.float32)[0].reshape(KC, 128).T)
    dww = np.ascontiguousarray(np.asarray(dw_w, np.float32)[0].reshape(31, 4, 128).transpose(2, 1, 0))
    cvec = np.zeros((128, 20), np.float32)
    cvec[:, 0:4] = np.asarray(dw_b, np.float32)[0][perm].T
    cvec[:, 4:8] = np.asarray(conv_ln_g, np.float32)[0][perm].T
    cvec[:, 8:12] = np.asarray(conv_ln_b, np.float32)[0][perm].T
    id32 = np.zeros((128, 32), np.float32)
    id32[np.arange(128), np.arange(128) % 32] = 0.5
    return {
        "w_in": w_in_l, "ng": ng, "w_out": w_out_l, "dww": dww, "cvec": cvec,
        "sink": np.ascontiguousarray(np.asarray(attn_sink, np.float32).reshape(1, 8)),
        "fg": np.ascontiguousarray(np.asarray(final_g, np.float32).reshape(1, D)),
        "ident": np.eye(128, dtype=np.float32), "id32": id32,
    }


def _core_inputs(core, x_prompt, x_sample):
    segs = []
    kb = np.zeros((128, 2 * NTS), np.float32)
    buf = np.zeros((SEGT, D), np.float32)
    buf[128:128 + 4096] = x_prompt[core]
    kb[:, 0] = NEG
    kb[:, NTS - 1] = NEG
    segs.append(buf)
    b, q = core // 4, core % 4
    lo = q * 4096
    buf = np.zeros((SEGT, D), np.float32)
    a0, a1 = lo - 128, lo + 4096 + 128
    s0, s1 = max(a0, 0), min(a1, x_sample.shape[1])
    buf[s0 - a0:s1 - a0] = x_sample[b, s0:s1]
    if a0 < 0:
        kb[:, NTS] = NEG
    if a1 > x_sample.shape[1]:
        kb[:, 2 * NTS - 1] = NEG
    segs.append(buf)
    xT = np.stack([np.ascontiguousarray(s.T.reshape(KC, 128, SEGT).transpose(1, 0, 2)) for s in segs])
    xtok = np.stack([segs[0][128:128 + 4096], segs[1][128:128 + 4096]])
    return {"xT": xT, "xtok": np.ascontiguousarray(xtok), "kbias": kb}


_NC_CACHE = {}


def kernel(x_prompt, x_sample, norm_g, w_in, attn_sink, dw_w, dw_b, conv_ln_g, conv_ln_b, w_out, final_g):
    x_prompt = np.asarray(x_prompt, np.float32)
    x_sample = np.asarray(x_sample, np.float32)
    wts = _prep_weights(norm_g, w_in, attn_sink, dw_w, dw_b, conv_ln_g, conv_ln_b, w_out, final_g)
    in_maps = []
    for core in range(8):
        m = _core_inputs(core, x_prompt, x_sample)
        m.update(wts)
        in_maps.append(m)
    if "nc" not in _NC_CACHE:
        _NC_CACHE["nc"] = build_nc()
    res = run_bass_kernel_spmd(_NC_CACHE["nc"], in_maps, core_ids=list(range(8)))
    y_prompt = np.empty_like(x_prompt)
    y_sample = np.empty_like(x_sample)
    for core in range(8):
        y = res.results[core]["y"]
        y_prompt[core] = y[0]
        b, q = core // 4, core % 4
        y_sample[b, q * 4096:(q + 1) * 4096] = y[1]
    return (y_prompt, y_sample)
```

```python
from contextlib import ExitStack

import numpy as np
import concourse.bass as bass
import concourse.mybir as mybir
from concourse.alu_op_type import AluOpType as ALU
from concourse.bass_utils import run_bass_kernel_spmd

F32 = mybir.dt.float32
BF16 = mybir.dt.bfloat16
I32 = mybir.dt.int32
AF = mybir.ActivationFunctionType

D = 1024
KC = 8
PW = 2816
NTS = 34
SEGT = NTS * 128
EPS = 1e-6
NEG = -1e30
ENGS = ("pe", "act", "dve", "pool", "sp")


class Sched:
    def __init__(self, nc, stack):
        self.nc = nc
        self.stack = stack
        self.prog = {e: [] for e in ENGS}
        self.sems = {}
        self.cnt = {}
        self.seen = {e: {} for e in ENGS}
        self.lastw = {}
        self.readers = {}
        for e in ("pe", "act", "dve", "pool"):
            self._sem(e)
        self.n_wait = {e: 0 for e in ENGS}
        self.n_inst = {e: 0 for e in ENGS}

    def _sem(self, key):
        if key not in self.sems:
            name = "s_" + "_".join(str(k) for k in (key if isinstance(key, tuple) else (key,)))
            self.sems[key] = self.stack.enter_context(self.nc.semaphore(name))
            self.cnt[key] = 0
        return self.sems[key]

    def _deps(self, eng, reads, writes):
        deps = {}

        def add(ev):
            if ev is None:
                return
            k, v = ev
            if deps.get(k, 0) < v:
                deps[k] = v

        for r in reads:
            excl = isinstance(r, tuple) and r[0] == "ps"
            add(self.lastw.get(r))
            if excl:
                for ev in self.readers.get(r, ()):
                    if ev[0] != eng:
                        add(ev)
        for w in writes:
            ev = self.lastw.get(w)
            if ev is not None and ev[0] != eng:
                add(ev)
            for ev in self.readers.get(w, ()):
                if ev[0] != eng:
                    add(ev)
        out = []
        for k, v in deps.items():
            if k == "pe" and eng == "pe":
                continue
            if self.seen[eng].get(k, 0) >= v:
                continue
            self.seen[eng][k] = v
            out.append((k, v))
        return out

    def _record(self, ev, reads, writes):
        for w in writes:
            self.lastw[w] = ev
            self.readers[w] = []
        for r in reads:
            if r in writes:
                continue
            lst = self.readers.setdefault(r, [])
            lst[:] = [e for e in lst if e[0] != ev[0]]
            lst.append(ev)

    def _emit_waits(self, eng, deps):
        for k, v in deps:
            sem = self.sems[k]
            self.prog[eng].append(I("wait_ge", sem, v))
            self.n_wait[eng] += 1

    def op(self, eng, fn, reads=(), writes=()):
        deps = self._deps(eng, reads, writes)
        self._emit_waits(eng, deps)
        self.cnt[eng] += 1
        v = self.cnt[eng]
        sem = self.sems[eng]
        self.prog[eng].append(lambda h, fn=fn, sem=sem: fn(h).then_inc(sem, 1))
        self.n_inst[eng] += 1
        self._record((eng, v), reads, writes)
        return (eng, v)

    def group(self, eng, fns, reads=(), writes=()):
        deps = self._deps(eng, reads, writes)
        self._emit_waits(eng, deps)
        for fn in fns[:-1]:
            self.prog[eng].append(lambda h, fn=fn: fn(h))
        self.cnt[eng] += 1
        v = self.cnt[eng]
        sem = self.sems[eng]
        last = fns[-1]
        self.prog[eng].append(lambda h, fn=last, sem=sem: fn(h).then_inc(sem, 1))
        self.n_inst[eng] += len(fns)
        self._record((eng, v), reads, writes)
        return (eng, v)

    def dma(self, q, semkey, fn, reads=(), writes=()):
        sem = self._sem(semkey)
        deps = self._deps(q, reads, writes)
        self._emit_waits(q, deps)
        self.cnt[semkey] += 16
        v = self.cnt[semkey]
        self.prog[q].append(lambda h, fn=fn, sem=sem: fn(h).then_inc(sem, 16))
        self.n_inst[q] += 1
        self._record((semkey, v), reads, writes)
        return (semkey, v)

    def wait_all(self, eng, events):
        deps = []
        for k, v in events:
            if self.seen[eng].get(k, 0) < v:
                self.seen[eng][k] = v
                deps.append((k, v))
        self._emit_waits(eng, deps)

    def finish(self):
        nc = self.nc
        with nc.Block() as block:
            @block.sync
            def _(h):
                for f in self.prog["sp"]:
                    f(h)

            @block.tensor
            def _(h):
                for f in self.prog["pe"]:
                    f(h)

            @block.scalar
            def _(h):
                for f in self.prog["act"]:
                    f(h)

            @block.vector
            def _(h):
                for f in self.prog["dve"]:
                    f(h)

            @block.gpsimd
            def _(h):
                for f in self.prog["pool"]:
                    f(h)


def I(method, *a, **k):
    return lambda h: getattr(h, method)(*a, **k)


def build_nc(n_seg=2, n_st=8):
    nc = bass.Bass("TRN2", target_bir_lowering=False)

    def din(name, shape, dt=F32):
        return nc.dram_tensor(name, list(shape), dt, kind="ExternalInput").ap()

    xT_d = din("xT", [2, 128, KC, SEGT])
    xtok_d = din("xtok", [2, 4096, D])
    kbias_d = din("kbias", [128, 2 * NTS])
    win_d = din("w_in", [128, KC, PW])
    ng_d = din("ng", [128, KC])
    wout_d = din("w_out", [128, KC, D])
    dww_d = din("dww", [128, 4, 31])
    cvec_d = din("cvec", [128, 20])
    sink_d = din("sink", [1, 8])
    fg_d = din("fg", [1, D])
    ident_d = din("ident", [128, 128])
    id32_d = din("id32", [128, 32])
    y_d = nc.dram_tensor("y", [2, 4096, D], F32, kind="ExternalOutput").ap()

    with ExitStack() as st:
        S = Sched(nc, st)

        def sb(name, shape, dt):
            return st.enter_context(nc.sbuf_tensor("sb_" + name, list(shape), dt))

        Wp = sb("Wp", [128, KC, PW], BF16)
        Wo = sb("Wo", [128, KC, D], BF16)
        DG = sb("DG", [128, 4, 31, 32], BF16)
        ET = sb("ET", [128, 3, 2, 512], BF16)
        identb = sb("identb", [128, 128], BF16)
        onesb = sb("onesb", [128, 128], BF16)
        fgb = sb("fgb", [128, D], F32)
        ng = sb("ng", [128, KC], F32)
        dww = sb("dww", [128, 4, 31], F32)
        cvec = sb("cvec", [128, 20], F32)
        esink = sb("esink", [128, 8], F32)
        kbias = sb("kbias", [128, 2 * NTS], F32)
        nhalf = sb("nhalf", [128, 1], F32)
        id32 = sb("id32", [128, 32], F32)

        XT = sb("XT", [128, KC, 512], F32)
        XN = sb("XN", [128, KC, 512], BF16)
        XSQ = sb("XSQ", [128, KC, 512], BF16)
        VBC = sb("VBC", [128, 512], F32)
        RBC = sb("RBC", [128, 512], F32)
        QT = sb("QT", [128, 4, 640], BF16)
        KT = sb("KT", [128, 8, 128], BF16)
        VP = sb("VP", [128, 8, 2, 65], BF16)
        HB = sb("HB", [128, 4, 784], BF16)
        SGC = sb("SGC", [128, 4, 640], BF16)
        SGA = sb("SGA", [128, 8, 512], BF16)
        TH = sb("TH", [128, 2, 512], F32)
        TH2 = sb("TH2", [128, 2, 512], F32)
        PT = sb("PT", [128, 2, 3, 512], BF16)
        MA = sb("MA", [128, 2, 512], BF16)
        TO = sb("TO", [128, 2, 256], F32)
        DEN = sb("DEN", [128, 2, 8], F32)
        MT = sb("MT", [128, KC, 512], BF16)
        YB = sb("YB", [128, 2, 512], BF16)
        YSQ = sb("YSQ", [128, 2, 512], BF16)
        MU = sb("MU", [128, 512], F32)
        VR = sb("VR", [128, 512], F32)
        T1 = sb("T1", [128, 2, 512], F32)
        LL = sb("LL", [128, 2, 512], F32)
        XTOK = sb("XTOK", [128, 3, D], F32)
        NWA = sb("NWA", [128, 512], F32)
        NWT = sb("NWT", [128, 512], I32)
        RR = sb("RR", [128, 512], F32)
        SS = sb("SS", [128, 4], F32)

        Df = TH2[:, 0, 0:128]
        Dabs = TH2[:, 0, 128:256]
        Mle = TH2[:, 0, 256:384]
        Mge = TH2[:, 0, 384:512]
        Etmp = TH2[:, 1, 0:128]
        identf = TH2[:, 1, 128:256]
        Di = TH2[:, 1, 256:384].bitcast(I32)
        JUNK = YB[:, :, :].rearrange("p a b -> p (a b)")
        PS = [st.enter_context(nc.psum_tensor("ps%d" % i, [128, 512], F32)) for i in range(8)]
        bank_ctr = [0]

        def nb():
            b = bank_ctr[0] % 8
            bank_ctr[0] += 1
            return b

        def pk(b):
            return ("ps", b)

        def ld(key, dst, src, wkey):
            S.dma("sp", key, I("dma_start", out=dst, in_=src), writes=[wkey])

        ld("c_ng", ng[:], ng_d, "ng")
        ld("c_dww", dww[:], dww_d, "dww")
        ld("c_cvec", cvec[:], cvec_d, "cvec")
        ld("c_kb", kbias[:], kbias_d, "kbias")
        ld("c_id", identf, ident_d, "identf")
        ld("c_id32", id32[:], id32_d, "id32")
        S.dma("sp", "c_sink", I("dma_start", out=esink[:], in_=sink_d.to_broadcast([128, 8])),
              writes=["esink"])
        S.dma("sp", "c_fg", I("dma_start", out=fgb[:], in_=fg_d.to_broadcast([128, D])),
              writes=["fgb"])

        S.op("pool", I("memset", nhalf[:], -0.5), writes=["nhalf"])
        S.op("pool", I("memset", onesb[:], 1.0), writes=["onesb"])
        S.op("pool", I("memset", VP[:, :, :, 64:65], 1.0), writes=[("VP", s) for s in range(8)])
        S.op("dve", I("tensor_copy", out=identb[:], in_=identf), reads=["identf"], writes=["identb"])
        S.op("act", I("activation", out=esink[:], in_=esink[:], func=AF.Exp),
             reads=["esink"], writes=["esink"])
        S.op("dve", I("tensor_scalar", out=cvec[:, 12:20], in0=cvec[:, 4:12], scalar1=0.5, scalar2=None,
                                              op0=ALU.mult), reads=["cvec"], writes=["cvec"])

        S.op("pool", I("iota", Di, pattern=[[1, 128]], base=0, channel_multiplier=-1), writes=["Di"])
        S.op("dve", I("tensor_copy", out=Df, in_=Di), reads=["Di"], writes=["Df"])
        S.op("act", I("activation", out=Dabs, in_=Df, func=AF.Abs), reads=["Df"], writes=["Dabs"])
        S.op("dve", I("tensor_scalar", out=Mle, in0=Df, scalar1=0.0, scalar2=None, op0=ALU.is_le),
             reads=["Df"], writes=["Mle"])
        S.op("dve", I("tensor_scalar", out=Mge, in0=Df, scalar1=0.0, scalar2=None, op0=ALU.is_ge),
             reads=["Df"], writes=["Mge"])
        for hd in range(8):
            m = 2.0 ** (-(hd + 1))
            g, a = hd // 4, hd % 4
            cs = slice(a * 128, (a + 1) * 128)
            S.op("act", I("activation", out=Etmp, in_=Df, func=AF.Exp, scale=-m, bias=-128.0 * m),
                 reads=["Df"], writes=["Etmp"])
            S.op("dve", I("tensor_tensor", out=ET[:, 0, g, cs], in0=Etmp, in1=Mle, op=ALU.mult),
                 reads=["Etmp", "Mle"], writes=["ET"])
            S.op("act", I("activation", out=ET[:, 1, g, cs], in_=Dabs, func=AF.Exp, scale=-m),
                 reads=["Dabs"], writes=["ET"])
            S.op("act", I("activation", out=Etmp, in_=Df, func=AF.Exp, scale=m, bias=-128.0 * m),
                 reads=["Df"], writes=["Etmp"])
            S.op("dve", I("tensor_tensor", out=ET[:, 2, g, cs], in0=Etmp, in1=Mge, op=ALU.mult),
                 reads=["Etmp", "Mge"], writes=["ET"])

        XTf = XT[:, 0:6, :].rearrange("p a b -> p (a b)")
        XKf = XTOK[:, :, :].rearrange("p a b -> p (a b)")
        for kc in range(KC):
            stg, skey, eng, dkey = (XTf, "XT", "dve", "w_st0") if kc % 2 == 0 else (XKf, "XTOKall", "pool", "w_st1")
            S.dma("sp", dkey, I("dma_start", out=stg[:, 0:PW], in_=win_d[:, kc, :]), writes=[skey])
            S.op(eng, I("tensor_scalar", out=Wp[:, kc, 0:512], in0=stg[:, 0:512], scalar1=ng[:, kc:kc + 1],
                        scalar2=0.125, op0=ALU.mult, op1=ALU.mult), reads=[skey, "ng"], writes=["Wp"])
            S.op(eng, I("tensor_scalar", out=Wp[:, kc, 512:PW], in0=stg[:, 512:PW], scalar1=ng[:, kc:kc + 1],
                        scalar2=1.0, op0=ALU.mult, op1=ALU.mult), reads=[skey, "ng"], writes=["Wp"])
        def build_wo():
            for q2 in range(4):
                S.dma("sp", "w_st1", I("dma_start", out=XKf[:, 0:2 * D].rearrange("p (k c) -> p k c", c=D),
                                       in_=wout_d[:, 2 * q2:2 * q2 + 2, :]), writes=["XTOKall"])
                S.op("dve", I("tensor_scalar", out=Wo[:, 2 * q2:2 * q2 + 2, :],
                              in0=XKf[:, 0:2 * D].rearrange("p (k c) -> p k c", c=D),
                              scalar1=(0.5 if q2 < 2 else 0.25), scalar2=None, op0=ALU.mult),
                     reads=["XTOKall"], writes=["Wo"])

        for c in range(4):
            S.op("pool", I("tensor_tensor", out=DG[:, c, :, :], in0=id32[:, :].unsqueeze(1).to_broadcast([128, 31, 32]),
                           in1=dww[:, c, :].unsqueeze(2).to_broadcast([128, 31, 32]), op=ALU.mult),
                 reads=["id32", "dww"], writes=["DG"])
        def rsqrt_bc(R, rk, V, vk, N):
            S.op("dve", I("tensor_scalar", out=NWT[:, 0:N], in0=V.bitcast(I32), scalar1=1, scalar2=None,
                          op0=ALU.arith_shift_right), reads=[vk], writes=["NWT"])
            S.op("dve", I("tensor_scalar", out=R.bitcast(I32), in0=NWT[:, 0:N], scalar1=-1.0, scalar2=1597463007.0,
                          op0=ALU.mult, op1=ALU.add), reads=["NWT"], writes=[rk])
            for _ in range(2):
                S.op("dve", I("tensor_tensor", out=NWA[:, 0:N], in0=R, in1=R, op=ALU.mult), reads=[rk], writes=["NWA"])
                S.op("dve", I("scalar_tensor_tensor", out=NWA[:, 0:N], in0=NWA[:, 0:N], scalar=-0.5, in1=V,
                              op0=ALU.mult, op1=ALU.mult), reads=["NWA", vk], writes=["NWA"])
                S.op("dve", I("scalar_tensor_tensor", out=R, in0=NWA[:, 0:N], scalar=1.5, in1=R,
                              op0=ALU.add, op1=ALU.mult), reads=["NWA", rk], writes=[rk])

        def slot(tile):
            return (tile - 1) % 8

        pool = {"banks": list(range(8)), "i": 0}

        def nb():
            lst = pool["banks"]
            b = lst[pool["i"] % len(lst)]
            pool["i"] += 1
            return b

        def set_pool(lst):
            pool["banks"] = list(lst)
            pool["i"] = 0

        thc = [0]

        def th_slot():
            thc[0] += 1
            return thc[0] % 2

        xtc = [0]
        ssc = [0]
        stores = {}

        def prep_early(seg, tok0, T):
            def s_a():
                S.dma("sp", "xT", I("dma_start", out=XT[:, :, 0:T], in_=xT_d[seg, :, :, tok0:tok0 + T]),
                      writes=["XT"])
                S.op("act", I("activation", out=XSQ[:, :, 0:T], in_=XT[:, :, 0:T], func=AF.Square),
                     reads=["XT"], writes=["XSQ"])

            def s_b():
                b = nb()
                S.group("pe", [I("matmul", PS[b][:, 0:T], lhsT=onesb[:], rhs=XSQ[:, kc, 0:T],
                                 start=(kc == 0), stop=(kc == KC - 1)) for kc in range(KC)],
                        reads=["XSQ", "onesb"], writes=[pk(b)])
                S.op("act", I("activation", out=VBC[:, 0:T], in_=PS[b][:, 0:T], func=AF.Identity,
                              scale=1.0 / D, bias=EPS), reads=[pk(b)], writes=["VBC"])

            def s_c():
                rsqrt_bc(RBC[:, 0:T], "RBC", VBC[:, 0:T], "VBC", T)
            return [s_a, s_b, s_c]

        def prep_xn(T):
            def s_x(k0):
                def f():
                    for kc in range(k0, k0 + 2):
                        S.op("dve", I("tensor_tensor", out=XN[:, kc, 0:T], in0=XT[:, kc, 0:T], in1=RBC[:, 0:T],
                                      op=ALU.mult), reads=["XT", "RBC"], writes=["XN"])
                return f
            return [s_x(k0) for k0 in range(0, KC, 2)]

        def fm_chunk(c0, T):
            b = nb()
            S.group("pe", [I("matmul", PS[b][:, 0:T], lhsT=Wp[:, kc, c0:c0 + 128], rhs=XN[:, kc, 0:T],
                             start=(kc == 0), stop=(kc == KC - 1)) for kc in range(KC)],
                    reads=["XN", "Wp"], writes=[pk(b)])
            return b

        def a_steps(seg, tile0, nt, full, qcol, hcol, shift=True):
            T = nt * 128
            main, gcs = [], []

            def s_shift():
                S.op("act", I("activation", out=QT[:, :, 0:128], in_=QT[:, :, 512:640], func=AF.Copy),
                     reads=["QT"], writes=["QT"])
                S.op("act", I("activation", out=HB[:, :, 0:144], in_=HB[:, :, 512:656], func=AF.Copy),
                     reads=["HB"], writes=["HB"])
            if full and shift:
                main.append(s_shift)

            def s_q(a):
                def f():
                    b = fm_chunk(a * 128, T)
                    S.op("act", I("activation", out=QT[:, a, qcol:qcol + T], in_=PS[b][:, 0:T], func=AF.Copy),
                         reads=[pk(b)], writes=["QT"])
                return f

            def s_k():
                b = fm_chunk(512, T)
                for i in range(nt):
                    s = slot(tile0 + i)
                    S.op("act", I("activation", out=KT[:, s, :], in_=PS[b][:, i * 128:(i + 1) * 128], func=AF.Copy),
                         reads=[pk(b)], writes=[("KT", s)])

            def s_glu(c):
                def f():
                    bv = fm_chunk(1280 + c * 128, T)
                    bg = fm_chunk(1792 + c * 128, T)
                    ts = th_slot()
                    S.op("act", I("activation", out=TH[:, ts, 0:T], in_=PS[bg][:, 0:T], func=AF.Tanh, scale=0.5),
                         reads=[pk(bg)], writes=[("TH", ts)])
                    S.op("dve", I("scalar_tensor_tensor", out=HB[:, c, hcol:hcol + T], in0=TH[:, ts, 0:T], scalar=1.0,
                                  in1=PS[bv][:, 0:T], op0=ALU.add, op1=ALU.mult),
                         reads=[("TH", ts), pk(bv)], writes=["HB"])
                return f

            def s_gc(c):
                def f():
                    b = fm_chunk(2304 + c * 128, T)
                    ts = th_slot()
                    S.op("act", I("activation", out=TH[:, ts, 0:T], in_=PS[b][:, 0:T], func=AF.Tanh, scale=0.5),
                         reads=[pk(b)], writes=[("TH", ts)])
                    S.op("dve", I("scalar_tensor_tensor", out=SGC[:, c, qcol:qcol + T], in0=TH[:, ts, 0:T], scalar=1.0,
                                  in1=PS[b][:, 0:T], op0=ALU.add, op1=ALU.mult),
                         reads=[("TH", ts), pk(b)], writes=["SGC"])
                return f

            def s_v():
                bV = nb()
                for i in range(nt):
                    S.group("pe", [I("matmul", PS[bV][:, i * 128:(i + 1) * 128], lhsT=XN[:, kc, i * 128:(i + 1) * 128],
                                     rhs=Wp[:, kc, 640:768], start=(kc == 0), stop=(kc == KC - 1))
                                   for kc in range(KC)], reads=["XN", "Wp"], writes=[pk(bV)])
                for i in range(nt):
                    s = slot(tile0 + i)
                    S.op("act", I("activation", out=VP[:, s, :, 0:64],
                                  in_=PS[bV][:, i * 128:(i + 1) * 128].rearrange("p (g d) -> p g d", g=2),
                                  func=AF.Copy), reads=[pk(bV)], writes=[("VP", s)])

            def s_ga(i):
                def f():
                    s = slot(tile0 + i)
                    b = nb()
                    S.group("pe", [I("matmul", PS[b][:, :], lhsT=XN[:, kc, i * 128:(i + 1) * 128],
                                     rhs=Wp[:, kc, 768:1280], start=(kc == 0), stop=(kc == KC - 1))
                                   for kc in range(KC)], reads=["XN", "Wp"], writes=[pk(b)])
                    ts = th_slot()
                    S.op("act", I("activation", out=TH[:, ts, :], in_=PS[b][:, :], func=AF.Tanh, scale=0.5),
                         reads=[pk(b)], writes=[("TH", ts)])
                    S.op("dve", I("scalar_tensor_tensor", out=SGA[:, s, :], in0=TH[:, ts, :], scalar=1.0,
                                  in1=PS[b][:, :], op0=ALU.add, op1=ALU.mult),
                         reads=[("TH", ts), pk(b)], writes=[("SGA", s)])
                return f

            main.append(s_k)
            main.append(s_v)
            if full:
                for a in range(4):
                    main.append(s_q(a))
                for i in range(nt):
                    main.append(s_ga(i))
            for c in range(4):
                f_ = s_glu(c)
                f_._w = 3.4
                main.append(f_)

            def s_shift_gc():
                S.op("act", I("activation", out=SGC[:, :, 0:128], in_=SGC[:, :, 512:640], func=AF.Copy),
                     reads=["SGC"], writes=["SGC"])
            if full:
                if shift:
                    gcs.append(s_shift_gc)
                for c in range(4):
                    gcs.append(s_gc(c))
            return main, gcs

        BS = [4, 5, 6]
        BO = 7

        def att_steps(seg, qt0, nq, qcol):
            steps = []

            def half_a(i, g):
                def f():
                    qt = qt0 + i
                    qc = qcol + i * 128
                    rows = slice(64 * g, 64 * g + 64)
                    for kb in range(3):
                        ks = slot(qt - 1 + kb)
                        S.op("pe", I("matmul", PS[BS[kb]][:, :], lhsT=KT[rows, ks, :], rhs=QT[rows, :, qc:qc + 128],
                                     start=True, stop=True), reads=[("KT", ks), "QT"], writes=[pk(BS[kb])])
                    for kb in range(3):
                        kcol = seg * NTS + qt - 1 + kb
                        S.op("act", I("activation", out=PT[:, g, kb, :], in_=PS[BS[kb]][:, :], func=AF.Exp,
                                      bias=kbias[:, kcol:kcol + 1]), reads=[pk(BS[kb]), "kbias"], writes=[("PT", g, kb)])
                        S.op("dve", I("tensor_tensor", out=PT[:, g, kb, :], in0=PT[:, g, kb, :], in1=ET[:, kb, g, :],
                                      op=ALU.mult), reads=[("PT", g, kb), "ET"], writes=[("PT", g, kb)])
                return f

            def half_b(i, g):
                def f():
                    qt = qt0 + i
                    ms = i % 2
                    fns = []
                    for a in range(4):
                        for kb in range(3):
                            ks = slot(qt - 1 + kb)
                            fns.append(I("matmul", PS[BO][:, a * 65:(a + 1) * 65], lhsT=PT[:, g, kb, a * 128:(a + 1) * 128],
                                         rhs=VP[:, ks, g, :], start=(kb == 0), stop=(kb == 2)))
                    S.group("pe", fns, reads=[("PT", g, kb) for kb in range(3)] +
                            [("VP", slot(qt - 1 + kb)) for kb in range(3)], writes=[pk(BO)])
                    Ov = PS[BO][:, 0:260].rearrange("p (a e) -> p a e", a=4)
                    S.op("dve", I("tensor_tensor", out=DEN[:, g, 0:4], in0=Ov[:, :, 64], in1=esink[:, 4 * g:4 * g + 4],
                                  op=ALU.add), reads=[pk(BO), "esink"], writes=[("DEN", g)])
                    S.op("dve", I("reciprocal", out=DEN[:, g, 4:8], in_=DEN[:, g, 0:4]),
                         reads=[("DEN", g)], writes=[("DEN", g)])
                    S.op("dve", I("tensor_tensor", out=TO[:, g, :].rearrange("p (a d) -> p a d", a=4), in0=Ov[:, :, 0:64],
                                  in1=DEN[:, g, 4:8].unsqueeze(2).to_broadcast([128, 4, 64]), op=ALU.mult),
                         reads=[pk(BO), ("DEN", g)], writes=[("TO", g)])
                    S.op("dve", I("tensor_tensor", out=MA[:, ms, 256 * g:256 * g + 256], in0=TO[:, g, :],
                                  in1=SGA[:, slot(qt), 256 * g:256 * g + 256], op=ALU.mult),
                         reads=[("TO", g), ("SGA", slot(qt))], writes=[("MA", ms, g)])
                return f

            def half_c(i):
                def f():
                    ms = i % 2
                    Tv = PS[BO][:].bitcast(BF16)
                    S.group("pe", [I("transpose", Tv[:, c * 128:(c + 1) * 128], MA[:, ms, c * 128:(c + 1) * 128],
                                     identb[:]) for c in range(4)],
                            reads=[("MA", ms, 0), ("MA", ms, 1), "identb"], writes=[pk(BO)])
                    S.op("act", I("activation", out=MT[:, 0:4, i * 128:(i + 1) * 128],
                                  in_=Tv[:, 0:512].rearrange("p (c q) -> p c q", c=4), func=AF.Copy),
                         reads=[pk(BO)], writes=[("MTa", i)])
                return f
            units = [(i, g) for i in range(nq) for g in range(2)]
            n = len(units)
            for u, (i, g) in enumerate(units):
                steps.append(half_a(i, g))
                if u >= 3 and u % 2 == 1:
                    steps.append(half_c((u - 3) // 2))
                if u >= 1:
                    steps.append(half_b(*units[u - 1]))
            steps.append(half_b(*units[-1]))
            if nq >= 2:
                pass
            done = set((u - 3) // 2 for u in range(n) if u >= 3 and u % 2 == 1)
            for i in range(nq):
                if i not in done:
                    steps.append(half_c(i))
            return steps

        BY = [0, 1, 2, 3]

        def conv_steps(hcol, N, npieces=8):
            taps = [(c, j) for j in range(31) for c in range(4)]
            per = (len(taps) + npieces - 1) // npieces
            steps = []

            def piece(lst):
                def f():
                    fns = []
                    for c, j in lst:
                        for q4 in range(4):
                            r = slice(32 * q4, 32 * q4 + 32)
                            fns.append(I("matmul", PS[BY[q4]][32 * c:32 * c + 32, 0:N], lhsT=DG[r, c, j, :],
                                         rhs=HB[r, c, hcol + j - 15:hcol + j - 15 + N],
                                         start=(j == 0), stop=(j == 30), tile_position=(32 * q4, 32 * c)))
                    S.group("pe", fns, reads=["HB", "DG"], writes=[pk(b) for b in BY])
                return f
            for k in range(0, len(taps), per):
                steps.append(piece(taps[k:k + per]))
            return steps

        def ln_steps(N, qcol):
            b1, b2 = 4, 5
            steps = []

            def stat_a(be):
                def f():
                    sl = be % 2
                    S.op("act", I("activation", out=YB[:, sl, 0:N], in_=PS[BY[be]][:, 0:N], func=AF.Identity,
                                  bias=cvec[:, be:be + 1]), reads=[pk(BY[be]), "cvec"], writes=[("YB", sl)])
                    S.op("act", I("activation", out=YSQ[:, sl, 0:N], in_=PS[BY[be]][:, 0:N], func=AF.Square,
                                  bias=cvec[:, be:be + 1]), reads=[pk(BY[be]), "cvec"], writes=[("YSQ", sl)])
                return f

            def stat_b(be):
                def f():
                    sl = be % 2
                    S.op("pe", I("matmul", PS[b1][:, 0:N], lhsT=onesb[:], rhs=YB[:, sl, 0:N], start=(be == 0),
                                 stop=(be == 3)), reads=[("YB", sl), "onesb"], writes=[pk(b1)])
                    S.op("pe", I("matmul", PS[b2][:, 0:N], lhsT=onesb[:], rhs=YSQ[:, sl, 0:N], start=(be == 0),
                                 stop=(be == 3)), reads=[("YSQ", sl), "onesb"], writes=[pk(b2)])
                return f

            def fin_a():
                S.op("act", I("activation", out=MU[:, 0:N], in_=PS[b1][:, 0:N], func=AF.Identity, scale=1.0 / 512),
                     reads=[pk(b1)], writes=["MU"])
                S.op("act", I("activation", out=VR[:, 0:N], in_=PS[b2][:, 0:N], func=AF.Identity, scale=1.0 / 512,
                              bias=EPS), reads=[pk(b2)], writes=["VR"])
                pool["banks"] = pool["banks"] + [b1, b2]

            def fin_d():
                S.op("dve", I("tensor_tensor", out=NWA[:, 0:N], in0=MU[:, 0:N], in1=MU[:, 0:N], op=ALU.mult),
                     reads=["MU"], writes=["NWA"])
                S.op("dve", I("tensor_tensor", out=VR[:, 0:N], in0=VR[:, 0:N], in1=NWA[:, 0:N], op=ALU.subtract),
                     reads=["VR", "NWA"], writes=["VR"])
                rsqrt_bc(RR[:, 0:N], "RR", VR[:, 0:N], "VR", N)

            tsl = {}

            def t1_d(be):
                def f():
                    sl = be % 2
                    S.op("dve", I("scalar_tensor_tensor", out=T1[:, sl, 0:N], in0=PS[BY[be]][:, 0:N],
                                  scalar=cvec[:, be:be + 1], in1=MU[:, 0:N], op0=ALU.add, op1=ALU.subtract),
                         reads=[pk(BY[be]), "cvec", "MU"], writes=[("T1", sl)])
                    S.op("dve", I("tensor_tensor", out=T1[:, sl, 0:N], in0=T1[:, sl, 0:N], in1=RR[:, 0:N], op=ALU.mult),
                         reads=[("T1", sl), "RR"], writes=[("T1", sl)])
                return f

            def t1_a(be):
                def f():
                    sl = be % 2
                    ts = be % 2
                    tsl[be] = ts
                    S.op("act", I("activation", out=LL[:, sl, 0:N], in_=T1[:, sl, 0:N], func=AF.Identity,
                                  scale=cvec[:, 4 + be:5 + be], bias=cvec[:, 8 + be:9 + be]),
                         reads=[("T1", sl), "cvec"], writes=[("LL", sl)])
                    S.op("act", I("activation", out=TH2[:, ts, 0:N], in_=T1[:, sl, 0:N], func=AF.Tanh,
                                  scale=cvec[:, 12 + be:13 + be], bias=cvec[:, 16 + be:17 + be]),
                         reads=[("T1", sl), "cvec"], writes=[("TH2", ts)])
                return f

            def t1_z(be):
                def f():
                    sl = be % 2
                    ts = tsl[be]
                    S.op("dve", I("scalar_tensor_tensor", out=LL[:, sl, 0:N], in0=TH2[:, ts, 0:N], scalar=1.0,
                                  in1=LL[:, sl, 0:N], op0=ALU.add, op1=ALU.mult),
                         reads=[("TH2", ts), ("LL", sl)], writes=[("LL", sl)])
                    S.op("dve", I("tensor_tensor", out=MT[:, 4 + be, 0:N], in0=LL[:, sl, 0:N],
                                  in1=SGC[:, be, qcol:qcol + N], op=ALU.mult),
                         reads=[("LL", sl), "SGC"], writes=[("MTc", be)])
                return f
            steps += [stat_a(0), stat_a(1), stat_b(0), stat_a(2), stat_b(1), stat_a(3), stat_b(2), stat_b(3)]
            steps += [fin_a, fin_d]
            steps += [t1_d(0), t1_d(1), t1_a(0), t1_d(2), t1_a(1), t1_z(0), t1_d(3), t1_a(2), t1_z(1), t1_a(3),
                      t1_z(2), t1_z(3)]
            return steps

        def outp_steps(seg, qt0, nq):
            tokbase = (qt0 - 1) * 128
            xsl = {}

            def load_x(i):
                sl = xtc[0] % 3
                xtc[0] += 1
                xsl[i] = sl
                t0 = tokbase + i * 128
                S.dma("sp", ("xl", sl), I("dma_start", out=XTOK[:, sl, :], in_=xtok_d[seg, t0:t0 + 128, :]),
                      writes=[("XTOK", sl), "XTOKall"] if xtc[0] <= 3 else [("XTOK", sl)])

            def tile(i):
                def f():
                    if i == 0:
                        load_x(0)
                    if i + 1 < nq:
                        load_x(i + 1)
                    sl = xsl[i]
                    t0 = tokbase + i * 128
                    bc = [nb(), nb()]
                    for hf in range(2):
                        S.group("pe", [I("matmul", PS[bc[hf]][:, :], lhsT=MT[:, kc, i * 128:(i + 1) * 128],
                                         rhs=Wo[:, kc, hf * 512:(hf + 1) * 512], start=(kc == 0), stop=(kc == KC - 1))
                                       for kc in range(KC)],
                                reads=[("MTa", i)] + [("MTc", be) for be in range(4)] + ["Wo"], writes=[pk(bc[hf])])
                    for hf in range(2):
                        S.op("dve", I("tensor_tensor", out=XTOK[:, sl, hf * 512:(hf + 1) * 512], in0=PS[bc[hf]][:, :],
                                      in1=XTOK[:, sl, hf * 512:(hf + 1) * 512], op=ALU.add),
                             reads=[pk(bc[hf]), ("XTOK", sl)], writes=[("XTOK", sl)])
                    sc = ssc[0] % 4
                    ssc[0] += 1
                    S.op("act", I("activation", out=JUNK, in_=XTOK[:, sl, :], func=AF.Square,
                                  accum_out=SS[:, sc:sc + 1]), reads=[("XTOK", sl)], writes=[("YB", 0), ("YB", 1), ("SS", sc)])
                    S.op("dve", I("tensor_scalar", out=SS[:, sc:sc + 1], in0=SS[:, sc:sc + 1], scalar1=1.0 / D,
                                  scalar2=EPS, op0=ALU.mult, op1=ALU.add), reads=[("SS", sc)], writes=[("SS", sc)])
                    S.op("pool", I("tensor_tensor", out=SS[:, sc:sc + 1], in0=SS[:, sc:sc + 1], in1=nhalf[:, 0:1],
                                   op=ALU.pow), reads=[("SS", sc), "nhalf"], writes=[("SS", sc)])
                    S.op("dve", I("scalar_tensor_tensor", out=XTOK[:, sl, :], in0=XTOK[:, sl, :], scalar=SS[:, sc:sc + 1],
                                  in1=fgb[:], op0=ALU.mult, op1=ALU.mult),
                         reads=[("XTOK", sl), ("SS", sc), "fgb"], writes=[("XTOK", sl)])
                    ev = S.dma("sp", ("st", sl), I("dma_start", out=y_d[seg, t0:t0 + 128, :], in_=XTOK[:, sl, :]),
                               reads=[("XTOK", sl)], writes=[])
                    stores[("st", sl)] = ev
                return f
            return [tile(i) for i in range(nq)]

        def run(steps):
            for f in steps:
                f()

        def merge_w(la, wa, lb, wb):
            if not la:
                return list(lb)
            if not lb:
                return list(la)
            def mids(w):
                tot = float(sum(w))
                out, c = [], 0.0
                for x in w:
                    out.append((c + 0.5 * x) / tot)
                    c += x
                return out
            ma, mb = mids(wa), mids(wb)
            out, i, k = [], 0, 0
            while i < len(la) or k < len(lb):
                if k >= len(lb) or (i < len(la) and ma[i] <= mb[k]):
                    out.append(la[i]); i += 1
                else:
                    out.append(lb[k]); k += 1
            return out

        def merge(*lists):
            lists = [l for l in lists if l]
            out = []
            pos = [0] * len(lists)
            total = sum(len(l) for l in lists)
            for _ in range(total):
                best, bi = None, None
                for k, l in enumerate(lists):
                    if pos[k] < len(l):
                        frac = (pos[k] + 0.5) / len(l)
                        if best is None or frac < best:
                            best, bi = frac, k
                out.append(lists[bi][pos[bi]])
                pos[bi] += 1
            return out

        def unit_prep(u):
            kind, sg, jj = u
            if kind == "st":
                return prep_early(sg, 128 + 512 * jj, 512), 512
            if kind == "haloR":
                return prep_early(sg, 128 + 512 * n_st, 128), 128
            if kind == "haloL":
                return prep_early(sg, 0, 128), 128
            return prep_early(sg, 128, 512), 512

        def unit_a(u):
            kind, sg, jj = u
            if kind == "st":
                return a_steps(sg, 4 * jj + 1, 4, True, 128, 144)
            if kind == "haloR":
                return a_steps(sg, 4 * n_st + 1, 1, False, None, 656)
            if kind == "haloL":
                return a_steps(sg, 0, 1, False, None, 16)
            raise AssertionError(kind)

        t_pending = None
        for seg in range(n_seg):
            last = 4 * n_st
            set_pool(range(8))
            if seg == 0:
                run(prep_early(seg, 0, 128) + prep_xn(128))
                m, _ = a_steps(seg, 0, 1, False, None, 16)
                run(merge(m, prep_early(seg, 128, 512)))
            run(prep_xn(512))
            m, g = a_steps(seg, 1, 4, True, 128, 144, shift=False)
            units = [("st", seg, jj) for jj in range(1, n_st)] + [("haloR", seg, None)]
            if seg + 1 < n_seg:
                units += [("haloL", seg + 1, None), ("st0", seg + 1, None)]
            pe0, t_pending = unit_prep(units[0])
            run(merge(m + g, pe0))
            if seg == 0:
                build_wo()
            for j in range(n_st + 1):
                if j == 0:
                    win = (1, 3, 128, 144)
                elif j < n_st:
                    win = (4 * j, 4, 0, 16)
                else:
                    win = (last, 1, 512, 528)
                qt0, nq, qcol, hcol = win
                N = nq * 128
                nxt = units[j] if j < len(units) else None
                nxt2 = units[j + 1] if j + 1 < len(units) else None
                if nxt is not None:
                    nxt_xn = prep_xn(t_pending)
                    nxt_a, nxt_g = unit_a(nxt)
                else:
                    nxt_xn, nxt_a, nxt_g = [], [], []
                if nxt2 is not None:
                    nxt2_prep, t_pending = unit_prep(nxt2)
                else:
                    nxt2_prep = []
                att = att_steps(seg, qt0, nq, qcol)
                ntail = min(len(att), 2 + (1 if nq >= 2 else 0))
                att_head, att_tail = att[:len(att) - ntail], att[len(att) - ntail:]
                p1 = merge(att_head, conv_steps(hcol, N))
                for r, stp in enumerate(nxt_xn):
                    p1.insert(min(len(p1), 1 + 2 * r), stp)
                run(p1)
                set_pool([6, 7])
                if nxt is not None and nxt[0] == "haloL":
                    run(att_tail)
                    att_tail = []
                lnl = ln_steps(N, qcol)
                w_ln = [1.5] * max(0, len(att_tail) - 1) + [1.0] * min(1, len(att_tail)) + \
                       [1.2, 1.2, 0.4, 1.2, 0.4, 1.2, 0.4, 0.4, 1.2, 7.0] + [1.2] * 12
                assert len(w_ln) == len(att_tail) + len(lnl)
                w_a = []
                for stp in nxt_a:
                    w_a.append(getattr(stp, "_w", 1.7))
                w_a += [0.3, 1.7, 0.3][:len(nxt2_prep)]
                run(merge_w(att_tail + lnl + [lambda: None], w_ln + [8.0], nxt_a + nxt2_prep, w_a))
                set_pool([4, 5, 6, 7])
                run(merge(outp_steps(seg, qt0, nq), nxt_g))

        S.wait_all("sp", list(stores.values()))
        S.finish()
        nc._sched_stats = (S.n_inst, S.n_wait)
    return nc


def _perm_bank():
    p = np.arange(128)
    return np.stack([128 * (p // 32) + 32 * be + (p % 32) for be in range(4)])


def _prep_weights(norm_g, w_in, attn_sink, dw_w, dw_b, conv_ln_g, conv_ln_b, w_out, final_g):
    w_in = np.asarray(w_in, np.float32)[0]
    w_out = np.asarray(w_out, np.float32)[0]
    perm = _perm_bank()
    cols = np.arange(PW)
    newq = np.empty(512, np.int64)
    for a in range(4):
        for g in range(2):
            newq[a * 128 + g * 64:a * 128 + g * 64 + 64] = (4 * g + a) * 64 + np.arange(64)
    cols[0:512] = newq
    cols[2304:2816] = 2304 + perm.reshape(-1)
    w_in_l = np.ascontiguousarray(w_in[:, cols].reshape(KC, 128, PW).transpose(1, 0, 2))
    rows = np.arange(D)
    rows[512:1024] = 512 + perm.reshape(-1)
    w_out_l = np.ascontiguousarray(w_out[rows, :].reshape(KC, 128, D).transpose(1, 0, 2))
    ng = np.ascontiguousarray(np.asarray(norm_g, np.float32)[0].reshape(KC, 128).T)
    dww = np.ascontiguousarray(np.asarray(dw_w, np.float32)[0].reshape(31, 4, 128).transpose(2, 1, 0))
    cvec = np.zeros((128, 20), np.float32)
    cvec[:, 0:4] = np.asarray(dw_b, np.float32)[0][perm].T
    cvec[:, 4:8] = np.asarray(conv_ln_g, np.float32)[0][perm].T
    cvec[:, 8:12] = np.asarray(conv_ln_b, np.float32)[0][perm].T
    id32 = np.zeros((128, 32), np.float32)
    id32[np.arange(128), np.arange(128) % 32] = 0.5
    return {
        "w_in": w_in_l, "ng": ng, "w_out": w_out_l, "dww": dww, "cvec": cvec,
        "sink": np.ascontiguousarray(np.asarray(attn_sink, np.float32).reshape(1, 8)),
        "fg": np.ascontiguousarray(np.asarray(final_g, np.float32).reshape(1, D)),
        "ident": np.eye(128, dtype=np.float32), "id32": id32,
    }


def _core_inputs(core, x_prompt, x_sample):
    segs = []
    kb = np.zeros((128, 2 * NTS), np.float32)
    buf = np.zeros((SEGT, D), np.float32)
    buf[128:128 + 4096] = x_prompt[core]
    kb[:, 0] = NEG
    kb[:, NTS - 1] = NEG
    segs.append(buf)
    b, q = core // 4, core % 4
    lo = q * 4096
    buf = np.zeros((SEGT, D), np.float32)
    a0, a1 = lo - 128, lo + 4096 + 128
    s0, s1 = max(a0, 0), min(a1, x_sample.shape[1])
    buf[s0 - a0:s1 - a0] = x_sample[b, s0:s1]
    if a0 < 0:
        kb[:, NTS] = NEG
    if a1 > x_sample.shape[1]:
        kb[:, 2 * NTS - 1] = NEG
    segs.append(buf)
    xT = np.stack([np.ascontiguousarray(s.T.reshape(KC, 128, SEGT).transpose(1, 0, 2)) for s in segs])
    xtok = np.stack([segs[0][128:128 + 4096], segs[1][128:128 + 4096]])
    return {"xT": xT, "xtok": np.ascontiguousarray(xtok), "kbias": kb}


_NC_CACHE = {}


def kernel(x_prompt, x_sample, norm_g, w_in, attn_sink, dw_w, dw_b, conv_ln_g, conv_ln_b, w_out, final_g):
    x_prompt = np.asarray(x_prompt, np.float32)
    x_sample = np.asarray(x_sample, np.float32)
    wts = _prep_weights(norm_g, w_in, attn_sink, dw_w, dw_b, conv_ln_g, conv_ln_b, w_out, final_g)
    in_maps = []
    for core in range(8):
        m = _core_inputs(core, x_prompt, x_sample)
        m.update(wts)
        in_maps.append(m)
    if "nc" not in _NC_CACHE:
        _NC_CACHE["nc"] = build_nc()
    res = run_bass_kernel_spmd(_NC_CACHE["nc"], in_maps, core_ids=list(range(8)))
    y_prompt = np.empty_like(x_prompt)
    y_sample = np.empty_like(x_sample)
    for core in range(8):
        y = res.results[core]["y"]
        y_prompt[core] = y[0]
        b, q = core // 4, core % 4
        y_sample[b, q * 4096:(q + 1) * 4096] = y[1]
    return (y_prompt, y_sample)
```
